# Optimizing a Trainium2 kernel written in Bass

```python
import math
import jax, jax.numpy as jnp
from jax import lax
import numpy as np

D_MODEL = 2048
BATCH = 32
SEQ = 256
DEPTH = 2
DEC_BATCH = 2
DEC_SEQ = 1024
PAST_LEN = 256

GRID_W = 64
N_BRANCH = 4
MIX_W = 512
ATT_HEADS = 8
ATT_KV_HEADS = 2
HEAD_DIM = 64
ATT_GROUP = ATT_HEADS // ATT_KV_HEADS
Q_BLOCK = 128
ROPE_THETA = 10000.0
FNET_GROUPS = 4
FNET_GW = MIX_W // FNET_GROUPS
SSM_HEADS = 8
SSM_HEAD_DIM = 64
SSM_GROUPS = 2
SSM_STATE = 128
SSM_CHUNK = 128
SSM_CONV = 3
SSM_D_INNER = SSM_HEADS * SSM_HEAD_DIM
SSM_CONV_CH = SSM_D_INNER + 2 * SSM_GROUPS * SSM_STATE
GMLP_GROUPS = 4
GMLP_CHUNK = 128
GMLP_GW = MIX_W // GMLP_GROUPS
D_FF = 4 * D_MODEL
EPS = 1e-6

IN_SPLITS = (ATT_HEADS * HEAD_DIM, ATT_KV_HEADS * HEAD_DIM, ATT_KV_HEADS * HEAD_DIM, MIX_W,
             SSM_CONV_CH, SSM_D_INNER, 2 * SSM_HEADS, 2 * MIX_W, N_BRANCH * D_MODEL)
N_IN = sum(IN_SPLITS)

kernel_name = "hybrid_diffusion_ctx_prefix_step"


def _rms(x):
    xf = x.astype(jnp.float32)
    return (xf * lax.rsqrt(jnp.mean(xf * xf, axis=-1, keepdims=True) + EPS)).astype(x.dtype)


def _rot_half(u, pos):
    half = u.shape[-1] // 2
    freqs = ROPE_THETA ** (-jnp.arange(half, dtype=jnp.float32) / half)
    ang = pos.astype(jnp.float32)[:, None] * freqs[None, :]
    cos = jnp.cos(ang)[None, :, None, :]
    sin = jnp.sin(ang)[None, :, None, :]
    uf = u.astype(jnp.float32)
    u1, u2 = uf[..., :half], uf[..., half:]
    return jnp.concatenate([u1 * cos - u2 * sin, u1 * sin + u2 * cos], axis=-1)


def _rope_2d(t, rows, cols):
    half = t.shape[-1] // 2
    out = jnp.concatenate([_rot_half(t[..., :half], rows), _rot_half(t[..., half:], cols)], axis=-1)
    return out.astype(t.dtype)


def _attend(q, k, v):
    b, L = q.shape[0], q.shape[1]
    nb = L // Q_BLOCK
    qb = jnp.moveaxis(q.reshape(b, nb, Q_BLOCK, ATT_KV_HEADS, ATT_GROUP, HEAD_DIM), 1, 0)
    scale = HEAD_DIM ** -0.5

    def block(qi):
        s = jnp.einsum('bqkgd,bskd->bkgqs', qi, k, preferred_element_type=jnp.float32) * scale
        p = jax.nn.softmax(s, axis=-1).astype(v.dtype)
        return jnp.einsum('bkgqs,bskd->bqkgd', p, v)

    o = lax.map(block, qb)
    return jnp.moveaxis(o, 0, 1).reshape(b, L, ATT_HEADS * HEAD_DIM)


def _fourier(f):
    b, L, _ = f.shape
    fg = f.reshape(b, L, FNET_GROUPS, FNET_GW).astype(jnp.float32)
    out = jnp.fft.fft2(fg, axes=(1, 3), norm="ortho").real
    return out.reshape(b, L, MIX_W).astype(f.dtype)


def _dwconv(u, w, bias):
    C = u.shape[-1]
    out = lax.conv_general_dilated(u, w.astype(u.dtype)[:, None, :], window_strides=(1,),
                                   padding=((SSM_CONV // 2, SSM_CONV // 2),),
                                   dimension_numbers=('NWC', 'WIO', 'NWC'),
                                   feature_group_count=C)
    return out + bias.astype(u.dtype)


def _ssd(x, dt, A, B, C, h0):
    b, L, h, p = x.shape
    n = B.shape[-1]
    nc = L // SSM_CHUNK
    Q = SSM_CHUNK
    x = x.reshape(b, nc, Q, h, p)
    dt = dt.reshape(b, nc, Q, h)
    B = B.reshape(b, nc, Q, h, n)
    C = C.reshape(b, nc, Q, h, n)
    a_cum = jnp.cumsum(dt * A, axis=2)
    mask = jnp.tril(jnp.ones((Q, Q), dtype=bool))[None, None, :, :, None]
    seg = a_cum[:, :, :, None, :] - a_cum[:, :, None, :, :]
    Lmat = jnp.exp(jnp.where(mask, seg, -jnp.inf))
    xdt = x * dt[..., None]
    cb = jnp.einsum('bcihn,bcjhn->bcijh', C, B)
    y_diag = jnp.einsum('bcijh,bcjhp->bcihp', cb * Lmat, xdt)
    decay_to_end = jnp.exp(a_cum[:, :, -1:, :] - a_cum)
    chunk_states = jnp.einsum('bcjhn,bcjh,bcjhp->bchpn', B, decay_to_end, xdt)
    chunk_decay = jnp.exp(a_cum[:, :, -1, :])

    def step(s, inp):
        st, dec = inp
        return s * dec[..., None, None] + st, s

    final, s_in = lax.scan(step, h0, (jnp.moveaxis(chunk_states, 1, 0), jnp.moveaxis(chunk_decay, 1, 0)))
    s_in = jnp.moveaxis(s_in, 0, 1)
    y_off = jnp.einsum('bcihn,bchpn,bcih->bcihp', C, s_in, jnp.exp(a_cum))
    return (y_diag + y_off).reshape(b, L, h, p), final


def _mamba(xbc, z, dt_raw, conv_w, conv_b, a_log, dt_bias, d_skip, norm_g, h0):
    b, L, _ = xbc.shape
    xbc = jax.nn.silu(_dwconv(xbc, conv_w, conv_b))
    xs, bm, cm = jnp.split(xbc, [SSM_D_INNER, SSM_D_INNER + SSM_GROUPS * SSM_STATE], axis=-1)
    hpg = SSM_HEADS // SSM_GROUPS
    xs = xs.reshape(b, L, SSM_HEADS, SSM_HEAD_DIM).astype(jnp.float32)
    bm = jnp.repeat(bm.reshape(b, L, SSM_GROUPS, SSM_STATE), hpg, axis=2).astype(jnp.float32)
    cm = jnp.repeat(cm.reshape(b, L, SSM_GROUPS, SSM_STATE), hpg, axis=2).astype(jnp.float32)
    dt = jax.nn.softplus(dt_raw.reshape(b, L, 2, SSM_HEADS).astype(jnp.float32) + dt_bias.astype(jnp.float32))
    A = -jnp.exp(a_log.astype(jnp.float32))
    h0 = h0.astype(jnp.float32)
    y_f, s_f = _ssd(xs, dt[:, :, 0], A[0], bm, cm, h0[:, 0])
    y_b, s_b = _ssd(jnp.flip(xs, 1), jnp.flip(dt[:, :, 1], 1), A[1], jnp.flip(bm, 1), jnp.flip(cm, 1), h0[:, 1])
    y = y_f + jnp.flip(y_b, 1) + d_skip.astype(jnp.float32)[:, None] * xs
    y = y.reshape(b, L, SSM_D_INNER) * jax.nn.silu(z.astype(jnp.float32))
    y = (_rms(y) * norm_g.astype(jnp.float32)).astype(z.dtype)
    return y, jnp.stack([s_f, s_b], axis=1)


def _gmlp(uv, norm_g, w_s, b_s):
    b, L, _ = uv.shape
    u, v = jnp.split(jax.nn.gelu(uv), 2, axis=-1)
    v = _rms(v) * norm_g
    nc = L // GMLP_CHUNK
    vc = v.reshape(b, nc, GMLP_CHUNK, GMLP_GROUPS, GMLP_GW)
    mixed = jnp.einsum('gij,bcjgd->bcigd', w_s, vc) + b_s.T[:, :, None]
    return u * mixed.reshape(b, L, MIX_W)


def _layer(x, cond, lw, h0, lat):
    (w_mod, b_mod, w_in, q_g, k_g, conv_w, conv_b, a_log, dt_bias, d_skip, ssm_g,
     gmlp_g, w_s, b_s, w_branch, w_out, w_ff1, w_ff2) = lw
    b, L, _ = x.shape
    mod = (jax.nn.silu(cond) @ w_mod + b_mod)[:, None, :]
    sh1, sc1, g1, sh2, sc2, g2 = jnp.split(mod, 6, axis=-1)
    h = _rms(x) * (1 + sc1) + sh1
    proj = h @ w_in
    q, k, v, f_in, xbc, z, dt_raw, uv, gate_logits = jnp.split(
        proj, np.cumsum(IN_SPLITS)[:-1].tolist(), axis=-1)
    q = _rms(q.reshape(b, L, ATT_HEADS, HEAD_DIM)) * q_g
    k = _rms(k.reshape(b, L, ATT_KV_HEADS, HEAD_DIM)) * k_g
    v = v.reshape(b, L, ATT_KV_HEADS, HEAD_DIM)
    if lat is None:
        q_use, keys, vals = q, k, v
    else:
        rows, cols, ctx_k, ctx_v = lat
        q_use = _rope_2d(q, rows, cols)
        keys = jnp.concatenate([ctx_k.astype(k.dtype), _rope_2d(k, rows, cols)], axis=1)
        vals = jnp.concatenate([ctx_v.astype(v.dtype), v], axis=1)
    o_att = _attend(q_use.reshape(b, L, ATT_KV_HEADS, ATT_GROUP, HEAD_DIM), keys, vals)
    o_fnet = _fourier(f_in)
    o_ssm, s_fin = _mamba(xbc, z, dt_raw, conv_w, conv_b, a_log, dt_bias, d_skip, ssm_g, h0)
    o_gmlp = _gmlp(uv, gmlp_g, w_s, b_s)
    br = jnp.stack([o_att, o_fnet, o_ssm.astype(o_att.dtype), o_gmlp], axis=2)
    gates = jax.nn.sigmoid(gate_logits.reshape(b, L, N_BRANCH, D_MODEL))
    merged = jnp.sum(gates * jnp.einsum('blkw,kwd->blkd', br, w_branch), axis=2)
    x = x + g1 * (merged @ w_out)
    h2 = _rms(x) * (1 + sc2) + sh2
    x = x + g2 * (jnp.square(jax.nn.relu(h2 @ w_ff1)) @ w_ff2)
    return x, k, v, s_fin


def setup_inputs(seed: int = 0) -> dict:
    key = jax.random.key(seed)
    ks = jax.random.split(key, 32)
    f32 = jnp.float32

    def nrm(k, shape, scale):
        return jax.random.normal(k, shape, f32) * scale

    dt0 = jnp.exp(jax.random.uniform(ks[14], (DEPTH, 2, SSM_HEADS), f32, math.log(1e-3), math.log(1e-1)))
    return {
        "x_prompt": nrm(ks[0], (BATCH, SEQ, D_MODEL), 1.0),
        "x_sample": nrm(ks[1], (DEC_BATCH, DEC_SEQ, D_MODEL), 1.0),
        "c": nrm(ks[2], (DEC_BATCH, D_MODEL), 1.0),
        "cache_k": nrm(ks[3], (DEC_BATCH, DEPTH, PAST_LEN, ATT_KV_HEADS, HEAD_DIM), 1.0),
        "cache_v": nrm(ks[4], (DEC_BATCH, DEPTH, PAST_LEN, ATT_KV_HEADS, HEAD_DIM), 1.0),
        "state_ssm": nrm(ks[5], (DEC_BATCH, DEPTH, 2, SSM_HEADS, SSM_HEAD_DIM, SSM_STATE), 0.5),
        "c_ctx": nrm(ks[6], (D_MODEL,), 1.0),
        "w_mod": nrm(ks[7], (DEPTH, D_MODEL, 6 * D_MODEL), 0.5 * D_MODEL ** -0.5),
        "b_mod": nrm(ks[8], (DEPTH, 6 * D_MODEL), 0.02),
        "w_in": nrm(ks[9], (DEPTH, D_MODEL, N_IN), D_MODEL ** -0.5),
        "q_norm_g": 1.0 + nrm(ks[10], (DEPTH, HEAD_DIM), 0.02),
        "k_norm_g": 1.0 + nrm(ks[11], (DEPTH, HEAD_DIM), 0.02),
        "conv_w": nrm(ks[12], (DEPTH, SSM_CONV, SSM_CONV_CH), SSM_CONV ** -0.5),
        "conv_b": nrm(ks[13], (DEPTH, SSM_CONV_CH), 0.02),
        "a_log": jnp.log(jax.random.uniform(ks[15], (DEPTH, 2, SSM_HEADS), f32, 1.0, 16.0)),
        "dt_bias": dt0 + jnp.log(-jnp.expm1(-dt0)),
        "d_skip": 1.0 + nrm(ks[16], (DEPTH, SSM_HEADS), 0.02),
        "ssm_norm_g": 1.0 + nrm(ks[17], (DEPTH, SSM_D_INNER), 0.02),
        "gmlp_norm_g": 1.0 + nrm(ks[18], (DEPTH, MIX_W), 0.02),
        "w_spatial": nrm(ks[19], (DEPTH, GMLP_GROUPS, GMLP_CHUNK, GMLP_CHUNK), GMLP_CHUNK ** -0.5),
        "b_spatial": 1.0 + nrm(ks[20], (DEPTH, GMLP_GROUPS, GMLP_CHUNK), 0.02),
        "w_branch": nrm(ks[21], (DEPTH, N_BRANCH, MIX_W, D_MODEL), MIX_W ** -0.5),
        "w_out": nrm(ks[22], (DEPTH, D_MODEL, D_MODEL), D_MODEL ** -0.5),
        "w_ff1": nrm(ks[23], (DEPTH, D_MODEL, D_FF), D_MODEL ** -0.5),
        "w_ff2": nrm(ks[24], (DEPTH, D_FF, D_MODEL), D_FF ** -0.5),
    }


def reference(x_prompt, x_sample, c, cache_k, cache_v, state_ssm, c_ctx, w_mod, b_mod, w_in,
              q_norm_g, k_norm_g, conv_w, conv_b, a_log, dt_bias, d_skip, ssm_norm_g,
              gmlp_norm_g, w_spatial, b_spatial, w_branch, w_out, w_ff1, w_ff2):
    n_rows = x_sample.shape[1] // GRID_W
    rows = jnp.repeat(jnp.arange(n_rows), GRID_W)
    cols = jnp.tile(jnp.arange(GRID_W), n_rows)
    cond_ctx = c_ctx[None, :]
    h0_ctx = jnp.zeros((x_prompt.shape[0], 2, SSM_HEADS, SSM_HEAD_DIM, SSM_STATE), jnp.float32)
    yp, ys = x_prompt, x_sample
    new_k, new_v, new_s = [], [], []
    for l in range(DEPTH):
        lw = (w_mod[l], b_mod[l], w_in[l], q_norm_g[l], k_norm_g[l], conv_w[l], conv_b[l],
              a_log[l], dt_bias[l], d_skip[l], ssm_norm_g[l], gmlp_norm_g[l], w_spatial[l],
              b_spatial[l], w_branch[l], w_out[l], w_ff1[l], w_ff2[l])
        yp, k_l, v_l, s_l = _layer(yp, cond_ctx, lw, h0_ctx, None)
        new_k.append(k_l)
        new_v.append(v_l)
        new_s.append(s_l)
        ys, _, _, _ = _layer(ys, c, lw, state_ssm[:, l], (rows, cols, cache_k[:, l], cache_v[:, l]))
    return (yp, ys, jnp.stack(new_k, axis=1), jnp.stack(new_v, axis=1), jnp.stack(new_s, axis=1))
```

```python
import numpy as np
from contextlib import ExitStack
import concourse.bass as bass
import concourse.mybir as mybir
from concourse.bass_utils import run_bass_kernel_spmd

F32 = mybir.dt.float32
BF16 = mybir.dt.bfloat16
AF = mybir.ActivationFunctionType
ALU = mybir.AluOpType
AX = mybir.AxisListType

D = 2048
T = 1280
NT = 10
TGS = [(0, 512), (512, 512), (1024, 256)]
NIN = 12048
DFF = 8192
EPS = 1e-6
OFF_Q, OFF_K, OFF_V, OFF_F, OFF_XBC, OFF_Z, OFF_DT, OFF_UV, OFF_G = 0, 512, 640, 768, 1280, 2304, 2816, 2832, 3856
NDS = 8
ENGS = ("pe", "act", "dve", "pool", "sp")


class Tok:
    __slots__ = ("w", "rs", "ep")

    def __init__(self):
        self.w = None
        self.rs = {}
        self.ep = -1


class Sched:
    def __init__(self, nc, es):
        self.nc = nc
        self.sem = {e: es.enter_context(nc.semaphore("sem_" + e)) for e in ENGS}
        self.cnt = {e: 0 for e in ENGS}
        self.dsem = {q: [es.enter_context(nc.semaphore("d%s%d" % (q, i))) for i in range(NDS)] for q in ("pool", "sp")}
        self.dcnt = {q: [0] * NDS for q in ("pool", "sp")}
        self.dnext = {"pool": 0, "sp": 0}
        self.ops = {e: [] for e in ENGS}
        self.waited = {e: {} for e in ENGS}
        self.inflight = {"pool": [], "sp": []}
        self.epoch = 0
        self.nblk = 0

    def _sync(self, t):
        if t.ep != self.epoch:
            t.w = None
            t.rs = {}
            t.ep = self.epoch

    def _need(self, eng, ev, waits):
        if ev is None:
            return
        key, sem, val = ev
        if key == "pe" and eng == "pe":
            return
        if self.waited[eng].get(key, 0) >= val:
            return
        if key in self.cnt:
            assert val <= self.cnt[key], "wait on a not-yet-emitted milestone (deadlock hazard): %s waits %s>=%d" % (eng, key, val)
        self.waited[eng][key] = val
        waits.append((sem, val))

    def _deps(self, eng, reads, writes, guard=()):
        waits = []
        for t in reads:
            self._sync(t)
            self._need(eng, t.w, waits)
        for t in list(writes) + list(guard):
            self._sync(t)
            self._need(eng, t.w, waits)
            for ev in t.rs.values():
                self._need(eng, ev, waits)
        return waits

    def op(self, eng, fn, reads=(), writes=(), sig=True, guard=()):
        waits = self._deps(eng, reads, writes, guard)
        if sig:
            self.cnt[eng] += 1
            ev = (eng, self.sem[eng], self.cnt[eng])
            for t in writes:
                t.w = ev
                t.rs = {}
            for t in reads:
                if t not in writes:
                    t.rs[eng] = ev
        else:
            assert not writes
            ev = (eng, self.sem[eng], self.cnt[eng] + 1)
            for t in reads:
                t.rs[eng] = ev
        self.ops[eng].append((waits, fn, "inc" if sig else None, None))

    def dma(self, q, out, in_, reads=(), writes=(), **kw):
        waits = self._deps(q, reads, writes)
        i = self.dnext[q]
        self.dnext[q] = (i + 1) % NDS
        key = "d%s%d" % (q, i)
        sem = self.dsem[q][i]
        if self.dcnt[q][i] > 0:
            self._need(q, (key, sem, 16 * self.dcnt[q][i]), waits)
        self.dcnt[q][i] += 1
        ev = (key, sem, 16 * self.dcnt[q][i])
        for t in writes:
            t.w = ev
            t.rs = {}
        for t in reads:
            t.rs[key] = ev
        self.inflight[q].append(ev)
        self.ops[q].append((waits, (lambda e, o=out, a=in_, k=kw: e.dma_start(out=o, in_=a, **k)), "dma", sem))

    def flush(self):
        nc = self.nc
        for q in ("pool", "sp"):
            waits = []
            for ev in self.inflight[q]:
                self._need(q, ev, waits)
            self.inflight[q] = []
            if waits:
                self.ops[q].append((waits, None, None, None))
        ops = self.ops
        self.ops = {e: [] for e in ENGS}
        semmap = self.sem

        def body(eng, lst):
            def f(e):
                for waits, fn, kind, extra in lst:
                    for sem, val in waits:
                        e.wait_ge(sem, val)
                    if fn is None:
                        continue
                    ins = fn(e)
                    if kind == "inc":
                        ins.then_inc(semmap[eng], 1)
                    elif kind == "dma":
                        ins.then_inc(extra, 16)
            return f

        self.nblk += 1
        with nc.Block() as block:
            if ops["pe"]:
                block.tensor(body("pe", ops["pe"]))
            if ops["act"]:
                block.scalar(body("act", ops["act"]))
            if ops["dve"]:
                block.vector(body("dve", ops["dve"]))
            if ops["pool"]:
                block.gpsimd(body("pool", ops["pool"]))
            if ops["sp"]:
                block.sync(body("sp", ops["sp"]))
        self.epoch += 1


def bc_mid(ap2, n):
    return ap2.unsqueeze(2).to_broadcast([ap2.shape[0], ap2.shape[1], n])


def bc_in(ap2, n):
    return ap2.unsqueeze(1).to_broadcast([ap2.shape[0], n, ap2.shape[1]])


def build(nlayers=2, taps=(), skip=(), upto=None, NL=2, ndbg=9):
    nc = bass.Bass("TRN2", target_bir_lowering=False)

    def din(name, shape, dt=F32):
        return nc.dram_tensor(name, list(shape), dt, kind="ExternalInput").ap()

    def dout(name, shape, dt=F32):
        return nc.dram_tensor(name, list(shape), dt, kind="ExternalOutput").ap()

    xin = din("xin", [T, D])
    cond = din("cond", [2, D])
    ctxk = din("ctxk", [2, 256, 128])
    ctxv = din("ctxv", [2, 256, 128])
    h0 = din("h0", [2, 2, 512, 128])
    flags = din("flags", [128, 4])
    ropec = din("ropec", [T, 64])
    ropes = din("ropes", [T, 64])
    kmask = din("kmask", [4, 1536])
    qmask = din("qmask", [4, T])
    dftA = din("dftA", [2, 1024, 1024])
    dftB = din("dftB", [2, 256, 256])
    dftC = din("dftC", [128, 256])
    identd = din("ident", [128, 128])
    trid = din("tri", [4, 128, 128])
    w_mod = din("w_mod", [NL, D, 6 * D])
    b_mod = din("b_mod", [NL, 6 * D])
    w_in = din("w_in", [NL, D, NIN])
    q_g = din("q_norm_g", [NL, 64])
    k_g = din("k_norm_g", [NL, 64])
    conv_w = din("conv_w", [NL, 3, 1024])
    conv_b = din("conv_b", [NL, 1024])
    a_log = din("a_log", [NL, 16])
    dt_bias = din("dt_bias", [NL, 16])
    d_skip = din("d_skip", [NL, 8])
    ssm_g = din("ssm_norm_g", [NL, 512])
    gmlp_g = din("gmlp_norm_g", [NL, 512])
    w_sp = din("w_spatial", [NL, 4, 128, 128])
    b_sp = din("b_spatial", [NL, 512])
    w_br = din("w_branch", [NL, 4, 512, D])
    w_out = din("w_out", [NL, D, D])
    w_ff1 = din("w_ff1", [NL, D, DFF])
    w_ff2 = din("w_ff2", [NL, DFF, D])

    y_o = dout("y", [T, D])
    nk_o = dout("nk", [2, T, 128])
    nv_o = dout("nv", [2, T, 128])
    ns_o = dout("ns", [2, 5, 2, 512, 128])
    xs = nc.dram_tensor("xs", [16, 128, T], F32, kind="Internal").ap()

    es = ExitStack()
    with es:
        KB = 512
        arena = es.enter_context(nc.sbuf_tensor("arena", [128, 192 * KB], BF16))
        cst = es.enter_context(nc.sbuf_tensor("cst", [128, 3420], F32))
        banks = [es.enter_context(nc.psum_tensor("ps%d" % i, [128, 512], F32)) for i in range(8)]
        ptok = [Tok() for _ in range(8)]
        S = Sched(nc, es)

        def A16(off_kb, n):
            o = int(round(off_kb * KB))
            return arena[:, o:o + n]

        def A32(off_kb, n):
            o = int(round(off_kb * KB))
            return arena[:, o:o + 2 * n].bitcast(F32)

        def bankbf(bk):
            return banks[bk][:, :].bitcast(BF16)

        c_off = [0]

        def calloc(n):
            o = c_off[0]
            c_off[0] += n
            assert c_off[0] <= 3420
            return cst[:, o:o + n]

        ident_f = calloc(128)
        tri = calloc(512).rearrange("p (a b) -> p a b", a=4)
        ones_f = calloc(128)
        flg = calloc(4)
        modT = calloc(2 * 2 * 96).rearrange("p (l s j) -> p l s j", l=2, s=2)
        identb = calloc(64).bitcast(BF16)
        onesb = calloc(64).bitcast(BF16)
        tribf = calloc(128).bitcast(BF16).rearrange("p (a b) -> p a b", a=2)
        epsc = calloc(1)
        small = calloc(1200)
        scT_c = calloc(16).bitcast(BF16).rearrange("p (s k) -> p s k", s=2)
        bmT_c = calloc(192).rearrange("p (l j) -> p l j", l=2)
        t_const = Tok()
        t_modl = [Tok(), Tok()]
        t_modc = Tok()

        tapn = [0]

        def tap(name, ap, toks):
            if name not in taps:
                return
            shp = list(ap.shape)
            d = nc.dram_tensor("dbg_" + name, shp, ap.dtype, kind="ExternalOutput").ap()
            S.dma("sp", d, ap, reads=toks)

        NR = 6
        ring = [A16(160 + 4 * i, 2048) for i in range(8)]
        rtok = [Tok() for _ in range(8)]
        rpos = [0]
        ring_n = [NR]

        def wload(w2d, nk, ncols=128):
            i = rpos[0] % ring_n[0]
            rpos[0] = (i + 1) % ring_n[0]
            v = ring[i][:, 0:nk * ncols].rearrange("p (k n) -> p k n", k=nk)
            S.dma("pool", v, w2d.rearrange("(k p) n -> p k n", p=128), writes=[rtok[i]])
            return v, rtok[i]

        pb = [0]

        def nextbank(lo=0, hi=7):
            b = lo + (pb[0] % (hi - lo))
            pb[0] += 1
            return b

        ev_alt = [0]

        def alt():
            ev_alt[0] ^= 1
            return "act" if ev_alt[0] else "dve"

        def copy_any(eng, o, i, reads, writes):
            if eng == "act":
                S.op("act", lambda e, o=o, i=i: e.copy(out=o, in_=i), reads=reads, writes=writes)
            else:
                S.op(eng, lambda e, o=o, i=i: e.tensor_copy(out=o, in_=i), reads=reads, writes=writes)

        def mm(o, a, b, st, sp, reads, wtok, gtok, fsig=False):
            S.op("pe", lambda e, o=o, a=a, b=b, st=st, sp=sp: e.matmul(o, lhsT=a, rhs=b, start=st, stop=sp),
                 reads=reads, writes=[wtok] if (sp and wtok is not None) else [],
                 guard=[gtok] if (st and gtok is not None) else [], sig=bool((sp and wtok is not None) or fsig))

        def tr(o, i, idn, reads, wtok, gtok, last, first):
            S.op("pe", lambda e, o=o, i=i, idn=idn: e.transpose(out=o, in_=i, identity=idn),
                 reads=list(reads) + [t_const], writes=[wtok] if last else [], guard=[gtok] if first else [], sig=last)

        def load_T(dst, src2d, R, tmp, tok):
            tt = Tok()
            S.dma("sp", tmp[0:R, :], src2d, writes=[tt])
            bk = nextbank()
            tr(banks[bk][:, 0:R], tmp[0:R, :], ident_f[0:R, 0:R], [tt], ptok[bk], ptok[bk], True, True)
            S.op("dve", lambda e, o=dst, i=banks[bk][:, 0:R]: e.tensor_copy(out=o, in_=i), reads=[ptok[bk]], writes=[tok])

        def p1(wfn, nblk, nk, rhs_fn, rtoks_fn, evac, tgs=TGS, modn=0):
            for blk in range(nblk):
                mod_drain(modn)
                wv, wt = wload(wfn(blk), nk)
                for tgi, (t0, n) in enumerate(tgs):
                    bk = nextbank()
                    for kc in range(nk):
                        mm(banks[bk][:, 0:n], wv[:, kc, :], rhs_fn(kc, t0, n), kc == 0, kc == nk - 1,
                           [wt] + rtoks_fn(kc), ptok[bk], ptok[bk])
                    evac(blk, tgi, t0, n, bk)

        S.dma("sp", ident_f, identd, writes=[t_const])
        S.dma("sp", tri, trid.rearrange("a p b -> p a b"), writes=[t_const])
        S.dma("sp", flg, flags, writes=[t_const])
        S.op("dve", lambda e: e.memset(ones_f, 1.0), writes=[t_const])
        S.op("dve", lambda e: e.memset(epsc, EPS), writes=[t_const])
        S.op("dve", lambda e: e.tensor_copy(out=identb, in_=ident_f), reads=[t_const], writes=[t_const])
        S.op("dve", lambda e: e.memset(onesb, 1.0), writes=[t_const])
        S.op("dve", lambda e: e.tensor_copy(out=tribf[:, 0, :], in_=tri[:, 0, :]), reads=[t_const], writes=[t_const])
        S.op("dve", lambda e: e.tensor_copy(out=tribf[:, 1, :], in_=tri[:, 2, :]), reads=[t_const], writes=[t_const])
        S.flush()

        xT = A32(80, 16 * T).rearrange("p (f t) -> p f t", f=16)
        t_xT = [Tok() for _ in range(16)]
        hT = A16(0, 16 * T).rearrange("p (f t) -> p f t", f=16)
        t_hT = [Tok() for _ in range(16)]
        oT = A16(80, 16 * T).rearrange("p (f t) -> p f t", f=16)
        t_oT = [Tok() for _ in range(16)]
        mgT = A16(40, 16 * T).rearrange("p (f t) -> p f t", f=16)
        t_mg = [Tok() for _ in range(16)]

        def stage0():
            xtok = [A32(0 + 8 * i, 2048) for i in range(2)]
            t_xtok = [Tok(), Tok()]
            mod_setup()
            mod_q.extend((0, j) for j in range(96))
            for t in range(NT):
                b = t % 2
                mod_drain(3)
                S.dma("sp", xtok[b], xin[t * 128:(t + 1) * 128, :], writes=[t_xtok[b]])
                for q4 in range(4):
                    bk = nextbank()
                    for j in range(4):
                        fc = q4 * 4 + j
                        tr(banks[bk][:, j * 128:(j + 1) * 128], xtok[b][:, fc * 128:(fc + 1) * 128], ident_f,
                           [t_xtok[b]], ptok[bk], ptok[bk], j == 3, j == 0)
                    dst = xT[:, q4 * 4:q4 * 4 + 4, t * 128:(t + 1) * 128]
                    src = banks[bk][:, :].rearrange("p (a b) -> p a b", a=4)
                    copy_any(alt(), dst, src, [ptok[bk]], t_xT[q4 * 4:q4 * 4 + 4])
            S.flush()

        def mod_setup():
            condT = A32(40, 32).rearrange("p (s k) -> p s k", s=2)
            load_T(A32(40, 32), cond.rearrange("s (k p) -> (s k) p", p=128), 32, A32(44, 128), t_modc)
            for l_ in range(nlayers):
                load_T(bmT_c[:, l_, :], b_mod[l_].rearrange("(j p) -> j p", p=128), 96, A32(45 + l_, 128), t_modc)
            S.op("act", lambda e: e.activation(out=scT_c, in_=condT, func=AF.Silu), reads=[t_modc], writes=[t_modc])

        mod_q = []

        def mod_block(l, j):
            wv, wt = wload(w_mod[l][:, j * 128:(j + 1) * 128], 16)
            for kc in range(16):
                mm(banks[7][:, 2 * j:2 * j + 2], wv[:, kc, :], scT_c[:, :, kc], kc == 0, kc == 15,
                   [wt, t_modc], ptok[7], ptok[7])
            if j % 16 == 15:
                w = j // 16
                pv = banks[7][:, 0:192].rearrange("p (j s) -> p j s", s=2)
                for s_ in range(2):
                    S.op("dve", lambda e, s_=s_: e.tensor_tensor(out=modT[:, l, s_, 16 * w:16 * w + 16], in0=pv[:, 16 * w:16 * w + 16, s_], in1=bmT_c[:, l, 16 * w:16 * w + 16], op=ALU.add),
                         reads=[ptok[7], t_modc], writes=[t_modl[l]])
                    if w in (1, 4):
                        S.op("dve", lambda e, s_=s_: e.tensor_scalar_add(out=modT[:, l, s_, 16 * w:16 * w + 16], in0=modT[:, l, s_, 16 * w:16 * w + 16], scalar1=1.0),
                             writes=[t_modl[l]])

        def mod_drain(n):
            for _ in range(n):
                if mod_q:
                    l_, j_ = mod_q.pop(0)
                    mod_block(l_, j_)

        def stage_norm(l, which):
            jsh = 0 if which == 0 else 48
            jsc = 16 if which == 0 else 64
            sq = [A16(40 + i, 512) for i in range(4)]
            t_sq = [Tok() for _ in range(4)]
            rstd = A32(44, T)
            t_rstd = Tok()
            tmpf = [A32(50 + 5 * i, T) for i in range(3)]
            t_tmp = [Tok(), Tok(), Tok()]
            k = 0
            for tgi, (t0, n) in enumerate(TGS):
                bk = nextbank()
                for fc in range(16):
                    b = k % 4
                    sqe = ("act", "dve", "act", "pool")[k % 4]
                    k += 1
                    if sqe == "act":
                        S.op("act", lambda e, o=sq[b][:, 0:n], i=xT[:, fc, t0:t0 + n]: e.activation(out=o, in_=i, func=AF.Square),
                             reads=[t_xT[fc]], writes=[t_sq[b]])
                    else:
                        S.op(sqe, lambda e, o=sq[b][:, 0:n], i=xT[:, fc, t0:t0 + n]: e.tensor_tensor(out=o, in0=i, in1=i, op=ALU.mult),
                             reads=[t_xT[fc]], writes=[t_sq[b]])
                    mm(banks[bk][:, 0:n], onesb, sq[b][:, 0:n], fc == 0, fc == 15, [t_sq[b], t_const], ptok[bk], ptok[bk], fsig=True)
                S.op("act", lambda e, o=rstd[:, t0:t0 + n], i=banks[bk][:, 0:n]: e.activation(out=o, in_=i, func=AF.Sqrt, scale=1.0 / D, bias=epsc),
                     reads=[ptok[bk], t_const], writes=[t_rstd])
                S.op("dve", lambda e, o=rstd[:, t0:t0 + n]: e.reciprocal(out=o, in_=o), reads=[t_rstd], writes=[t_rstd])
            if ndbg < 2:
                return
            for fc in range(16):
                b = fc % 3
                S.op("pool" if fc in (1, 4, 7, 10, 13, 15) else "dve", lambda e, o=tmpf[b], i=xT[:, fc, :]: e.tensor_tensor(out=o, in0=i, in1=rstd, op=ALU.mult),
                     reads=[t_xT[fc], t_rstd], writes=[t_tmp[b]])
                if ndbg < 3:
                    continue
                for s, (c0, c1) in enumerate(((0, 1024), (1024, 1280))):
                    S.op("act", lambda e, o=hT[:, fc, c0:c1], i=tmpf[b][:, c0:c1], sc=modT[:, l, s, jsc + fc:jsc + fc + 1], sh=modT[:, l, s, jsh + fc:jsh + fc + 1]:
                         e.activation(out=o, in_=i, func=AF.Identity, scale=sc, bias=sh),
                         reads=[t_tmp[b], t_modl[l]], writes=[t_hT[fc]])

        def spill_x():
            if "spill" in skip:
                return
            for fc in range(16):
                S.dma("sp", xs[fc], xT[:, fc, :], reads=[t_xT[fc]])

        def reload_x():
            for fc in range(16):
                S.dma("sp", xT[:, fc, :], xs[fc], writes=[t_xT[fc]])

        def stage_merge(l):
            Gs = [A16(120 + i, 512) for i in range(3)]
            t_Gs = [Tok() for _ in range(3)]
            acc = A32(124, T)
            t_acc = [Tok() for _ in range(3)]
            tmp = [A32(130 + 2 * i, 512) for i in range(3)]
            t_tmpm = [Tok() for _ in range(3)]
            kk = 0
            ring_n[0] = 8
            for fc in range(16):
                for k in range(4):
                    mod_drain(1 if k < 3 else 0)
                    wg, wgt = wload(w_in[l][:, OFF_G + k * D + fc * 128: OFF_G + k * D + (fc + 1) * 128], 16)
                    wb, wbt = wload(w_br[l, k][:, fc * 128:(fc + 1) * 128], 4)
                    for tgi, (t0, n) in enumerate(TGS):
                        bg = nextbank()
                        for kc in range(16):
                            mm(banks[bg][:, 0:n], wg[:, kc, :], hT[:, kc, t0:t0 + n], kc == 0, kc == 15, [wgt, t_hT[kc]], ptok[bg], ptok[bg])
                        bp = nextbank()
                        for kc in range(4):
                            mm(banks[bp][:, 0:n], wb[:, kc, :], oT[:, 4 * k + kc, t0:t0 + n], kc == 0, kc == 3, [wbt, t_oT[4 * k + kc]], ptok[bp], ptok[bp])
                        b = kk % 3
                        kk += 1
                        S.op("act", lambda e, o=Gs[b][:, 0:n], i=banks[bg][:, 0:n]: e.activation(out=o, in_=i, func=AF.Sigmoid),
                             reads=[ptok[bg]], writes=[t_Gs[b]])
                        a_ = acc[:, t0:t0 + n]
                        if k == 0:
                            S.op("dve", lambda e, o=a_, i0=banks[bp][:, 0:n], i1=Gs[b][:, 0:n]: e.tensor_tensor(out=o, in0=i0, in1=i1, op=ALU.mult),
                                 reads=[ptok[bp], t_Gs[b]], writes=[t_acc[tgi]])
                        else:
                            S.op("dve", lambda e, o=tmp[b][:, 0:n], i0=banks[bp][:, 0:n], i1=Gs[b][:, 0:n]: e.tensor_tensor(out=o, in0=i0, in1=i1, op=ALU.mult),
                                 reads=[ptok[bp], t_Gs[b]], writes=[t_tmpm[b]])
                            if k < 3:
                                S.op("dve", lambda e, o=a_, i1=tmp[b][:, 0:n]: e.tensor_tensor(out=o, in0=o, in1=i1, op=ALU.add),
                                     reads=[t_tmpm[b]], writes=[t_acc[tgi]])
                            else:
                                S.op("dve", lambda e, o=mgT[:, fc, t0:t0 + n], i0=a_, i1=tmp[b][:, 0:n]: e.tensor_tensor(out=o, in0=i0, in1=i1, op=ALU.add),
                                     reads=[t_tmpm[b], t_acc[tgi]], writes=[t_mg[fc]])
            S.flush()
            ring_n[0] = NR
            rpos[0] = 0

        def stage_wffn(l, lim=99):
            reload_x()

            def ev_res(jg):
                def f(blk, tgi, t0, n, bk):
                    s = 0 if tgi < 2 else 1
                    S.op("dve", lambda e, o=xT[:, blk, t0:t0 + n], i=banks[bk][:, 0:n], sc=modT[:, l, s, jg + blk:jg + blk + 1]:
                         e.scalar_tensor_tensor(out=o, in0=i, scalar=sc, in1=o, op0=ALU.mult, op1=ALU.add),
                         reads=[ptok[bk], t_modl[l]], writes=[t_xT[blk]])
                return f

            p1(lambda blk: w_out[l][:, blk * 128:(blk + 1) * 128], 16, 16,
               lambda kc, t0, n: mgT[:, kc, t0:t0 + n], lambda kc: [t_mg[kc]], ev_res(32))
            S.flush()
            if lim < 6:
                return
            stage_norm(l, 1)
            S.flush()
            if lim < 7:
                return
            aT = A16(40, 16 * T).rearrange("p (f t) -> p f t", f=16)
            t_aT = [Tok() for _ in range(16)]
            rl = [A32(0, 1) for _ in range(0)]
            rtmp = [A32(184 + 2 * i, 512) for i in range(3)]
            t_rt = [Tok() for _ in range(3)]
            cnt = [0]
            for q in range(4):
                def ev_ff1(blk, tgi, t0, n, bk):
                    b = cnt[0] % 3
                    cnt[0] += 1
                    S.op("act", lambda e, o=rtmp[b][:, 0:n], i=banks[bk][:, 0:n]: e.activation(out=o, in_=i, func=AF.Relu),
                         reads=[ptok[bk]], writes=[t_rt[b]])
                    eng = "dve"
                    S.op(eng, lambda e, o=aT[:, blk, t0:t0 + n], i=rtmp[b][:, 0:n]: e.tensor_tensor(out=o, in0=i, in1=i, op=ALU.mult),
                         reads=[t_rt[b]], writes=[t_aT[blk]])
                p1(lambda blk: w_ff1[l][:, q * D + blk * 128: q * D + (blk + 1) * 128], 16, 16,
                   lambda kc, t0, n: hT[:, kc, t0:t0 + n], lambda kc: [t_hT[kc]], ev_ff1, modn=(2 if q == 0 else 1))
                p1(lambda blk: w_ff2[l][q * D:(q + 1) * D, blk * 128:(blk + 1) * 128], 16, 16,
                   lambda kc, t0, n: aT[:, kc, t0:t0 + n], lambda kc: [t_aT[kc]], ev_res(80), modn=1)
            S.flush()

        bctr = {}

        def nb(key, lo, hi):
            c = bctr.get(key, 0)
            bctr[key] = c + 1
            return lo + c % (hi - lo)

        def bcast_row(dst, row_ap, n, tok):
            tt = bctr.setdefault("rowtok_small", Tok())
            tmp = small[0:1, 1100:1100 + n] if n <= 64 else None
            assert tmp is not None
            S.dma("sp", tmp, row_ap, writes=[tt])
            bk = nextbank()
            mm(banks[bk][:, 0:n], ones_f[0:1, :], tmp, True, True, [tt, t_const], ptok[bk], ptok[bk])
            S.op("dve", lambda e, o=dst, i=banks[bk][:, 0:n]: e.tensor_copy(out=o, in_=i), reads=[ptok[bk]], writes=[tok])

        def bcast_big(dst, row_ap, n, tok, tmpreg):
            tt = bctr.setdefault("rowtok_big", Tok())
            S.dma("sp", tmpreg[0:1, 0:n], row_ap, writes=[tt])
            bk = nextbank()
            mm(banks[bk][:, 0:n], ones_f[0:1, :], tmpreg[0:1, 0:n], True, True, [tt, t_const], ptok[bk], ptok[bk])
            S.op("dve", lambda e, o=dst, i=banks[bk][:, 0:n]: e.tensor_copy(out=o, in_=i), reads=[ptok[bk]], writes=[tok])

        def stage_attn(l):
            wq = A16(40, 16 * 768).rearrange("p (k n) -> p k n", k=16)
            t_wq = [Tok() for _ in range(3)]
            for c3 in range(3):
                S.dma("pool", wq[:, :, c3 * 256:(c3 + 1) * 256],
                      w_in[l][:, c3 * 256:(c3 + 1) * 256].rearrange("(k p) n -> p k n", p=128), writes=[t_wq[c3]])
            COS = A32(120, 640).rearrange("p (t d) -> p t d", t=10)
            SIN = A32(122.5, 640).rearrange("p (t d) -> p t d", t=10)
            gqk = A32(125, 640).rearrange("p (h d) -> p h d", h=10)
            t_rope = Tok()
            t_g = Tok()
            S.dma("sp", COS, ropec.rearrange("(t p) d -> p t d", p=128), writes=[t_rope])
            S.dma("sp", SIN, ropes.rearrange("(t p) d -> p t d", p=128), writes=[t_rope])
            gq = small[:, 0:64]
            gk = small[:, 64:128]
            bcast_row(gq, q_g[l:l + 1, :], 64, t_g)
            bcast_row(gk, k_g[l:l + 1, :], 64, t_g)
            S.op("dve", lambda e: e.tensor_copy(out=gqk[:, 0:8, :], in_=bc_in(gq, 8)), reads=[t_g], writes=[t_g])
            S.op("dve", lambda e: e.tensor_copy(out=gqk[:, 8:10, :], in_=bc_in(gk, 2)), reads=[t_g], writes=[t_g])
            qT = A16(127.5, 8 * T)[0:68, :].rearrange("p (h t) -> p h t", h=8)
            kT = A16(147.5, 2 * 1536)[0:68, :].rearrange("p (h t) -> p h t", h=2)
            vaug = A16(153.5, 12 * 130).rearrange("p (b k d) -> p b k d", b=12, k=2)
            o_tok = A16(64, 10 * 512).rearrange("p (t d) -> p t d", t=10)
            t_qT, t_kT, t_va, t_ot = Tok(), Tok(), Tok(), [Tok() for _ in range(10)]
            t_qm, t_km = Tok(), Tok()
            kctx = A16(116, 256).rearrange("p (b d) -> p b d", b=2)
            t_kc = Tok()
            S.dma("pool", kctx, ctxk[l].rearrange("(b p) d -> p b d", p=128), writes=[t_kc])
            for blk in range(2):
                S.dma("pool", vaug[:, blk, :, 0:64], ctxv[l][blk * 128:(blk + 1) * 128, :].rearrange("p (k d) -> p k d", k=2), writes=[t_va])
            S.op("dve", lambda e: e.memset(vaug[:, :, :, 64:65], 1.0), writes=[t_va])
            for kv in range(2):
                S.dma("pool", kT[64:68, kv, :], kmask, writes=[t_km])
            for h in range(8):
                S.dma("pool", qT[64:68, h, :], qmask, writes=[t_qm])
            bk = nextbank()
            for blk in range(2):
                for kv in range(2):
                    i4 = blk * 2 + kv
                    tr(bankbf(bk)[0:64, i4 * 128:(i4 + 1) * 128], kctx[:, blk, kv * 64:(kv + 1) * 64], identb,
                       [t_kc], ptok[bk], ptok[bk], i4 == 3, i4 == 0)
            for blk in range(2):
                S.op("dve", lambda e, o=kT[0:64, :, blk * 128:(blk + 1) * 128], i=bankbf(bk)[0:64, blk * 256:(blk + 1) * 256].rearrange("p (k c) -> p k c", k=2):
                     e.tensor_copy(out=o, in_=i), reads=[ptok[bk]], writes=[t_kT])
            def a1_mm(t):
                par = t % 2
                bq, bkv = 4 * par, 4 * par + 1
                cols = slice(t * 128, (t + 1) * 128)
                for kc in range(16):
                    mm(banks[bq][:, :], hT[:, kc, cols], wq[:, kc, 0:512], kc == 0, kc == 15, [t_hT[kc], t_wq[0], t_wq[1]], ptok[bq], ptok[bq])
                for kc in range(16):
                    mm(banks[bkv][:, 0:256], hT[:, kc, cols], wq[:, kc, 512:768], kc == 0, kc == 15, [t_hT[kc], t_wq[2]], ptok[bkv], ptok[bkv])

            def a1_rest(t, phase):
                par = t % 2
                base = 90 + 13 * par
                sq = A32(base, 640)
                qn = A32(base + 2.5, 640)
                Aa = A32(base + 5, 640)
                Bm = A32(base + 7.5, 640)
                qr = A16(base + 10, 640)
                vt = A32(base + 11.25, 128)
                ss = A32(base + 11.75, 16)[:, 0:10]
                rs = A32(base + 11.875, 16)[:, 0:10]
                tk = bctr.setdefault(("attok", par), [Tok() for _ in range(8)])
                t_sq, t_qn, t_Aa, t_Bm, t_qr, t_vt, t_ss, t_rs = tk
                bq, bkv, btq, btk = 4 * par, 4 * par + 1, 4 * par + 2, 4 * par + 3
                cols = slice(t * 128, (t + 1) * 128)
                if phase == 1:
                    S.op("act", lambda e, o=sq[:, 0:512], i=banks[bq][:, :]: e.activation(out=o, in_=i, func=AF.Square), reads=[ptok[bq]], writes=[t_sq])
                    S.op("act", lambda e, o=sq[:, 512:640], i=banks[bkv][:, 0:128]: e.activation(out=o, in_=i, func=AF.Square), reads=[ptok[bkv]], writes=[t_sq])
                    S.op("dve", lambda e, o=ss, i=sq.rearrange("p (h d) -> p h d", h=10): e.reduce_sum(out=o, in_=i, axis=AX.X), reads=[t_sq], writes=[t_ss])
                    S.op("act", lambda e, o=rs, i=ss: e.activation(out=o, in_=i, func=AF.Sqrt, scale=1.0 / 64, bias=epsc), reads=[t_ss, t_const], writes=[t_rs])
                    S.op("dve", lambda e, o=rs: e.reciprocal(out=o, in_=o), reads=[t_rs], writes=[t_rs])
                    S.op("dve", lambda e, o=qn[:, 0:512].rearrange("p (h d) -> p h d", h=8), i=banks[bq][:, :].rearrange("p (h d) -> p h d", h=8), r=bc_mid(rs[:, 0:8], 64):
                         e.tensor_tensor(out=o, in0=i, in1=r, op=ALU.mult), reads=[ptok[bq], t_rs], writes=[t_qn])
                    S.op("dve", lambda e, o=qn[:, 512:640].rearrange("p (h d) -> p h d", h=2), i=banks[bkv][:, 0:128].rearrange("p (h d) -> p h d", h=2), r=bc_mid(rs[:, 8:10], 64):
                         e.tensor_tensor(out=o, in0=i, in1=r, op=ALU.mult), reads=[ptok[bkv], t_rs], writes=[t_qn])
                    S.op("pool", lambda e, o=qn, g_=gqk.rearrange("p h d -> p (h d)"): e.tensor_tensor(out=o, in0=o, in1=g_, op=ALU.mult), reads=[t_g], writes=[t_qn])
                    S.dma("sp", nk_o[l, t * 128:(t + 1) * 128, :], qn[:, 512:640], reads=[t_qn])
                    S.op("act", lambda e, o=vt, i=banks[bkv][:, 128:256]: e.copy(out=o, in_=i), reads=[ptok[bkv]], writes=[t_vt])
                    S.dma("sp", nv_o[l, t * 128:(t + 1) * 128, :], vt, reads=[t_vt])
                    S.op("dve", lambda e, o=vaug[:, 2 + t, :, 0:64], i=banks[bkv][:, 128:256].rearrange("p (k d) -> p k d", k=2): e.tensor_copy(out=o, in_=i),
                         reads=[ptok[bkv]], writes=[t_va])
                if phase == 1:
                    return
                qn3 = qn.rearrange("p (h d) -> p h d", h=10)
                S.op("pool", lambda e, o=Aa.rearrange("p (h d) -> p h d", h=10), i=qn3, c_=bc_in(COS[:, t, :], 10): e.tensor_tensor(out=o, in0=i, in1=c_, op=ALU.mult),
                     reads=[t_qn, t_rope], writes=[t_Aa])
                qn5 = qn.rearrange("p (h r a d) -> p h r a d", h=10, r=2, a=2)
                Bm5 = Bm.rearrange("p (h r a d) -> p h r a d", h=10, r=2, a=2)
                sn4 = SIN[:, t, :].rearrange("p (r a d) -> p r a d", r=2, a=2)
                for a_ in range(2):
                    S.op("dve", lambda e, o=Bm5[:, :, :, a_, :], i=qn5[:, :, :, 1 - a_, :], s_=sn4[:, :, a_, :].unsqueeze(1).to_broadcast([128, 10, 2, 16]):
                         e.tensor_tensor(out=o, in0=i, in1=s_, op=ALU.mult), reads=[t_qn, t_rope], writes=[t_Bm])
                S.op("dve", lambda e, o=qr, i0=Aa, i1=Bm: e.tensor_tensor(out=o, in0=i0, in1=i1, op=ALU.add), reads=[t_Aa, t_Bm], writes=[t_qr])
                for h in range(8):
                    tr(bankbf(btq)[0:64, h * 128:(h + 1) * 128], qr[:, h * 64:(h + 1) * 64], identb, [t_qr], ptok[btq], ptok[btq], h == 7, h == 0)
                for kv in range(2):
                    tr(bankbf(btk)[0:64, kv * 128:(kv + 1) * 128], qr[:, 512 + kv * 64:512 + (kv + 1) * 64], identb, [t_qr], ptok[btk], ptok[btk], kv == 1, kv == 0)
                S.op("act", lambda e, o=qT[0:64, :, cols], i=bankbf(btq)[0:64, :].rearrange("p (h c) -> p h c", h=8): e.copy(out=o, in_=i),
                     reads=[ptok[btq]], writes=[t_qT])
                S.op("dve", lambda e, o=kT[0:64, :, 256 + t * 128:256 + (t + 1) * 128], i=bankbf(btk)[0:64, 0:256].rearrange("p (h c) -> p h c", h=2): e.tensor_copy(out=o, in_=i),
                     reads=[ptok[btk]], writes=[t_kT])

            a1_mm(0)
            a1_mm(1)
            a1_rest(0, 1)
            for t in range(NT):
                if t + 2 < NT:
                    a1_mm(t + 2)
                if t + 1 < NT:
                    a1_rest(t + 1, 1)
                a1_rest(t, 2)
            tap("qT", qT, [t_qT, t_qm])
            tap("kT", kT, [t_kT, t_km])
            S.flush()
            PT = [A16(40 + 10 * i, 10 * 512).rearrange("p (b q) -> p b q", b=10) for i in range(2)]
            t_PT = [[Tok() for _ in range(10)] for _ in range(2)]
            rd = [small[:, 216 + 4 * i:220 + 4 * i] for i in range(2)]
            t_rd = [Tok(), Tok()]
            it = 0
            for h in range(8):
                kv = h // 4
                for (q0, n, kbs, K) in ((0, 512, list(range(10)), 68), (512, 512, list(range(10)), 68), (1024, 256, [10, 11], 64)):
                    buf = it % 2
                    it += 1
                    for kbi, kb in enumerate(kbs):
                        bs = nb("sc", 0, 4)
                        mm(banks[bs][:, 0:n], kT[0:K, kv, kb * 128:(kb + 1) * 128], qT[0:K, h, q0:q0 + n], True, True,
                           [t_kT, t_qT, t_km, t_qm], ptok[bs], ptok[bs])
                        S.op("act", lambda e, o=PT[buf][:, kbi, 0:n], i=banks[bs][:, 0:n]: e.activation(out=o, in_=i, func=AF.Exp, scale=0.125),
                             reads=[ptok[bs]], writes=[t_PT[buf][kbi]])
                    bo = nb("pv", 4, 8)
                    nq = n // 128
                    for qs in range(nq):
                        for kbi, kb in enumerate(kbs):
                            first = (qs == 0 and kbi == 0)
                            last = (qs == nq - 1 and kbi == len(kbs) - 1)
                            S.op("pe", lambda e, o=banks[bo][:, qs * 128:qs * 128 + 65], a=PT[buf][:, kbi, qs * 128:(qs + 1) * 128], b=vaug[:, kb, kv, :], st=(kbi == 0), sp=(kbi == len(kbs) - 1):
                                 e.matmul(o, lhsT=a, rhs=b, start=st, stop=sp),
                                 reads=[t_PT[buf][kbi], t_va], writes=[ptok[bo]] if last else [], guard=[ptok[bo]] if first else [], sig=last)
                    pv3 = banks[bo][:, :].rearrange("p (q d) -> p q d", q=4)
                    S.op("dve", lambda e, o=rd[buf][:, 0:nq].unsqueeze(2), i=pv3[:, 0:nq, 64:65]: e.reciprocal(out=o, in_=i), reads=[ptok[bo]], writes=[t_rd[buf]])
                    t0i = q0 // 128
                    S.op("dve", lambda e, o=o_tok[:, t0i:t0i + nq, h * 64:(h + 1) * 64], i=pv3[:, 0:nq, 0:64], r=bc_mid(rd[buf][:, 0:nq], 64):
                         e.tensor_tensor(out=o, in0=i, in1=r, op=ALU.mult), reads=[ptok[bo], t_rd[buf]], writes=t_ot[t0i:t0i + nq])
            for t in range(NT):
                bk = nextbank()
                for j in range(4):
                    tr(bankbf(bk)[:, j * 128:(j + 1) * 128], o_tok[:, t, j * 128:(j + 1) * 128], identb, [t_ot[t]], ptok[bk], ptok[bk], j == 3, j == 0)
                copy_any(alt(), oT[:, 0:4, t * 128:(t + 1) * 128], bankbf(bk)[:, 0:512].rearrange("p (j c) -> p j c", j=4), [ptok[bk]], t_oT[0:4])
            S.flush()

        def stage_fourier(l):
            fT = A16(40, 4 * T).rearrange("p (g t) -> p g t", g=4)
            t_fT = [Tok() for _ in range(4)]
            G = A16(50, 10 * 1024).rearrange("p (t g c) -> p t g c", t=10, g=4)
            t_G = [Tok() for _ in range(10)]
            CS = A16(70, 256)
            TA = A16(120, 8 * 2 * 1024).rearrange("p (b c n) -> p b c n", b=8, c=2)
            TB = A16(152, 2 * 2 * 256).rearrange("p (b c n) -> p b c n", b=2, c=2)
            t_cs, t_ta, t_tb = Tok(), [Tok(), Tok()], Tok()
            def ev_f(blk, tgi, t0, n, bk):
                copy_any(alt(), fT[:, blk, t0:t0 + n], banks[bk][:, 0:n], [ptok[bk]], [t_fT[blk]])

            p1(lambda g: w_in[l][:, OFF_F + g * 128:OFF_F + (g + 1) * 128], 4, 16,
               lambda kc, t0, n: hT[:, kc, t0:t0 + n], lambda kc: [t_hT[kc]], ev_f)
            S.dma("pool", CS, dftC, writes=[t_cs])
            t_ta = [[Tok() for _ in range(2)] for _ in range(8)]
            for b_ in range(8):
                for c in range(2):
                    S.dma("pool", TA[:, b_, c, :], dftA[c][b_ * 128:(b_ + 1) * 128, :], writes=[t_ta[b_][c]])
            for c in range(2):
                S.dma("pool", TB[:, :, c, :], dftB[c].rearrange("(b p) n -> p b n", p=128), writes=[t_tb])

            for t in range(NT):
                for half in range(2):
                    bk = nextbank()
                    for gi in range(2):
                        g = half * 2 + gi
                        mm(banks[bk][:, gi * 256:(gi + 1) * 256], fT[:, g, t * 128:(t + 1) * 128], CS, True, True, [t_fT[g], t_cs],
                           ptok[bk] if gi == 1 else None, ptok[bk] if gi == 0 else None)
                    copy_any(alt(), G[:, t, half * 2:half * 2 + 2, :], banks[bk][:, :].rearrange("p (g c) -> p g c", g=2), [ptok[bk]], [t_G[t]])
            for g in range(4):
                for half in range(2):
                    bk = nextbank()
                    i = 0
                    for lt in range(8):
                        for c in range(2):
                            mm(banks[bk][:, :], G[:, lt, g, c * 128:(c + 1) * 128], TA[:, lt, c, half * 512:(half + 1) * 512], i == 0, i == 15,
                               [t_G[lt], t_ta[lt][c]], ptok[bk], ptok[bk])
                            i += 1
                    copy_any(alt(), oT[:, 4 + g, half * 512:(half + 1) * 512], banks[bk][:, :], [ptok[bk]], [t_oT[4 + g]])
                bk = nextbank()
                i = 0
                for lt in range(2):
                    for c in range(2):
                        mm(banks[bk][:, 0:256], G[:, 8 + lt, g, c * 128:(c + 1) * 128], TB[:, lt, c, :], i == 0, i == 3, [t_G[8 + lt], t_tb], ptok[bk], ptok[bk])
                        i += 1
                copy_any(alt(), oT[:, 4 + g, 1024:1280], banks[bk][:, 0:256], [ptok[bk]], [t_oT[4 + g]])
            S.flush()

        def gelu_block(ps, n, out_ap, tA, tB, tokA, tokB, ptk, wtoks):
            S.op("act", lambda e, o=tA[:, 0:n], i=ps: e.activation(out=o, in_=i, func=AF.Square), reads=[ptk], writes=[tokA])
            S.op("dve", lambda e, o=tA[:, 0:n]: e.tensor_scalar(out=o, in0=o, scalar1=0.044715, scalar2=1.0, op0=ALU.mult, op1=ALU.add), writes=[tokA])
            S.op("dve", lambda e, o=tA[:, 0:n], i=ps: e.tensor_tensor(out=o, in0=o, in1=i, op=ALU.mult), reads=[ptk], writes=[tokA])
            S.op("act", lambda e, o=tB[:, 0:n], i=tA[:, 0:n]: e.activation(out=o, in_=i, func=AF.Sigmoid, scale=1.5957691216057308), reads=[tokA], writes=[tokB])
            S.op("dve", lambda e, o=out_ap, i0=tB[:, 0:n], i1=ps: e.tensor_tensor(out=o, in0=i0, in1=i1, op=ALU.mult), reads=[tokB, ptk], writes=wtoks)

        def stage_gmlp(l):
            uT = A16(40, 4 * T).rearrange("p (g t) -> p g t", g=4)
            t_uT = [Tok() for _ in range(4)]
            v_tok = A16(50, 10 * 512).rearrange("p (t d) -> p t d", t=10)
            t_v = [Tok() for _ in range(10)]
            wsT = A16(60, 512).rearrange("p (g i) -> p g i", g=4)
            bsbc = A32(61, 512)
            ggm = A32(63, 512)
            wsl = A32(65, 512).rearrange("p (g j) -> p g j", g=4)
            gt = [(A32(67 + 4 * i, 512), A32(69 + 4 * i, 512), Tok(), Tok()) for i in range(3)]
            wv = A16(120, 16 * 512).rearrange("p (k n) -> p k n", k=16)
            t_wv = [Tok(), Tok()]
            vg = [A32(136 + 2 * i, 512) for i in range(2)]
            t_vg = [Tok(), Tok()]
            sqj = [A32(140, 512), A32(148, 512)]
            t_sqj = [Tok(), Tok()]
            tm = [A32(142 + 2 * i, 512) for i in range(2)]
            t_tm = [Tok(), Tok()]
            rowt = A32(146, 512)
            ssq = [small[:, 224 + 2 * i:225 + 2 * i] for i in range(2)]
            t_ssq = [Tok(), Tok()]
            t_ws, t_bs, t_gg = Tok(), Tok(), Tok()
            for c2 in range(2):
                S.dma("pool", wv[:, :, c2 * 256:(c2 + 1) * 256],
                      w_in[l][:, OFF_UV + 512 + c2 * 256:OFF_UV + 512 + (c2 + 1) * 256].rearrange("(k p) n -> p k n", p=128), writes=[t_wv[c2]])
            S.dma("sp", wsl, w_sp[l].rearrange("g i j -> i g j"), writes=[t_ws])
            bcast_big(bsbc, b_sp[l:l + 1, :], 512, t_bs, rowt)
            bcast_big(ggm, gmlp_g[l:l + 1, :], 512, t_gg, rowt)
            bk = nextbank()
            for g in range(4):
                tr(banks[bk][:, g * 128:(g + 1) * 128], wsl[:, g, :], ident_f, [t_ws], ptok[bk], ptok[bk], g == 3, g == 0)
            t_wsT = Tok()
            S.op("dve", lambda e, o=wsT.rearrange("p g i -> p (g i)"), i=banks[bk][:, :]: e.tensor_copy(out=o, in_=i), reads=[ptok[bk]], writes=[t_wsT])
            gi_ = [0]

            def ev_u(blk, tgi, t0, n, bk):
                a, b, ta, tb_ = gt[gi_[0] % 3]
                gi_[0] += 1
                gelu_block(banks[bk][:, 0:n], n, uT[:, blk, t0:t0 + n], a, b, ta, tb_, ptok[bk], [t_uT[blk]])
                vstep()

            def gv1(t):
                b2 = t % 2
                bk = nextbank()
                for kc in range(16):
                    mm(banks[bk][:, :], hT[:, kc, t * 128:(t + 1) * 128], wv[:, kc, :], kc == 0, kc == 15, [t_hT[kc], t_wv[0], t_wv[1]], ptok[bk], ptok[bk])
                a, b, ta, tb_ = gt[gi_[0] % 3]
                gi_[0] += 1
                gelu_block(banks[bk][:, :], 512, vg[b2], a, b, ta, tb_, ptok[bk], [t_vg[b2]])

            def gv2(t):
                b2 = t % 2
                S.op("act", lambda e, o=sqj[b2], i=vg[b2]: e.activation(out=o, in_=i, func=AF.Square), reads=[t_vg[b2]], writes=[t_sqj[b2]])
                S.op("dve", lambda e, o=ssq[b2], i=sqj[b2]: e.reduce_sum(out=o, in_=i, axis=AX.X), reads=[t_sqj[b2]], writes=[t_ssq[b2]])
                S.op("act", lambda e, o=ssq[b2]: e.activation(out=o, in_=o, func=AF.Sqrt, scale=1.0 / 512, bias=epsc), reads=[t_const], writes=[t_ssq[b2]])
                S.op("dve", lambda e, o=ssq[b2]: e.reciprocal(out=o, in_=o), writes=[t_ssq[b2]])
                S.op("dve", lambda e, o=v_tok[:, t, :], i=vg[b2], sc=ssq[b2], g_=ggm: e.scalar_tensor_tensor(out=o, in0=i, scalar=sc, in1=g_, op0=ALU.mult, op1=ALU.mult),
                     reads=[t_vg[b2], t_ssq[b2], t_gg], writes=[t_v[t]])


            vstate = [0]

            def vstep():
                t = vstate[0]
                if t > NT:
                    return
                if t == 0:
                    gv1(0)
                else:
                    if t < NT:
                        gv1(t)
                    gv2(t - 1)
                vstate[0] += 1

            p1(lambda g: w_in[l][:, OFF_UV + g * 128:OFF_UV + (g + 1) * 128], 4, 16,
               lambda kc, t0, n: hT[:, kc, t0:t0 + n], lambda kc: [t_hT[kc]], ev_u)
            while vstate[0] <= NT:
                vstep()
            for c in range(NT):
                b2 = c % 2
                bk = nextbank()
                for g in range(4):
                    mm(banks[bk][:, g * 128:(g + 1) * 128], v_tok[:, c, g * 128:(g + 1) * 128], wsT[:, g, :], True, True, [t_v[c], t_wsT],
                       ptok[bk] if g == 3 else None, ptok[bk] if g == 0 else None)
                S.op("dve", lambda e, o=tm[b2], i=banks[bk][:, :], b_=bsbc: e.tensor_tensor(out=o, in0=i, in1=b_, op=ALU.add), reads=[ptok[bk], t_bs], writes=[t_tm[b2]])
                S.op("pool", lambda e, o=oT[:, 12:16, c * 128:(c + 1) * 128], i=tm[b2].rearrange("p (g i) -> p g i", g=4), u_=uT[:, :, c * 128:(c + 1) * 128]:
                     e.tensor_tensor(out=o, in0=i, in1=u_, op=ALU.mult), reads=[t_tm[b2]] + t_uT, writes=t_oT[12:16])
            S.flush()

        def stage_ssm(l):
            samp = flg[:, 0:1]
            nsamp = flg[:, 1:2]
            convw = small[:, 128:152].rearrange("p (k c) -> p k c", k=3)
            convb = small[:, 152:160]
            nw0 = small[:, 160:168]
            nw2 = small[:, 168:176]
            dtb = small[:, 176:192]
            Abc = small[:, 192:208]
            dsk = small[:, 208:216]
            wgt = small[:, 232:248]
            dtv_all = small[:, 256:416].rearrange("p (t d) -> p t d", t=10)
            dtA_all = small[:, 416:576].rearrange("p (t d) -> p t d", t=10)
            E_all = small[:, 576:1056].rearrange("p (t d) -> p t d", t=10)
            t_cw, t_dtp, t_dt, t_E = Tok(), Tok(), [Tok() for _ in range(10)], [Tok() for _ in range(10)]
            xbcT = A16(40, 8 * T).rearrange("p (c t) -> p c t", c=8)
            t_xbc = [Tok() for _ in range(8)]
            z_tok = A16(60, 10 * 512).rearrange("p (t d) -> p t d", t=10)
            t_z = [Tok() for _ in range(10)]
            gssm = A32(70, 512)
            t_gs = Tok()
            raw = [A32(120 + 5 * i, T) for i in range(2)]
            acc = [A32(130 + 5 * i, T) for i in range(2)]
            t_raw, t_acc = [Tok(), Tok()], [Tok(), Tok()]
            wz = A16(140, 16 * 528).rearrange("p (k n) -> p k n", k=16)
            t_wz = [Tok(), Tok(), Tok()]
            rowt = A32(157, 512)
            for c3, (a, b) in enumerate(((0, 256), (256, 512), (512, 528))):
                S.dma("pool", wz[:, :, a:b], w_in[l][:, OFF_Z + a:OFF_Z + b].rearrange("(k p) n -> p k n", p=128), writes=[t_wz[c3]])
            load_T(small[:, 128:152], conv_w[l].rearrange("k (c p) -> (k c) p", p=128), 24, A32(100, 128), t_cw)
            load_T(convb, conv_b[l].rearrange("(c p) -> c p", p=128), 8, A32(101, 128), t_cw)
            S.op("dve", lambda e: e.tensor_scalar(out=nw0, in0=convw[:, 0, :], scalar1=nsamp, scalar2=-1.0, op0=ALU.mult, op1=ALU.mult), reads=[t_cw, t_const], writes=[t_cw])
            S.op("dve", lambda e: e.tensor_scalar(out=nw2, in0=convw[:, 2, :], scalar1=nsamp, scalar2=-1.0, op0=ALU.mult, op1=ALU.mult), reads=[t_cw, t_const], writes=[t_cw])
            bcast_row(dtb, dt_bias[l:l + 1, :], 16, t_dtp)
            bcast_row(Abc, a_log[l:l + 1, :], 16, t_dtp)
            S.op("act", lambda e: e.activation(out=Abc, in_=Abc, func=AF.Exp), writes=[t_dtp])
            S.op("dve", lambda e: e.tensor_scalar_mul(out=Abc, in0=Abc, scalar1=-1.0), writes=[t_dtp])
            bcast_row(dsk, d_skip[l:l + 1, :], 8, t_dtp)
            bcast_big(gssm, ssm_g[l:l + 1, :], 512, t_gs, rowt)

            def ev_x(blk, tgi, t0, n, bk):
                b = blk % 2
                S.op("act", lambda e, o=raw[b][:, t0:t0 + n], i=banks[bk][:, 0:n]: e.copy(out=o, in_=i), reads=[ptok[bk]], writes=[t_raw[b]])
                if tgi < 2:
                    return
                r, a = raw[b], acc[b]
                S.op("dve", lambda e: e.tensor_scalar(out=a, in0=r, scalar1=convw[:, 1, blk:blk + 1], scalar2=convb[:, blk:blk + 1], op0=ALU.mult, op1=ALU.add),
                     reads=[t_raw[b], t_cw], writes=[t_acc[b]])
                for (s0, s1) in ((0, 1024), (1024, 1280)):
                    S.op("dve", lambda e, o=a[:, s0 + 1:s1], i=r[:, s0:s1 - 1]: e.scalar_tensor_tensor(out=o, in0=i, scalar=convw[:, 0, blk:blk + 1], in1=o, op0=ALU.mult, op1=ALU.add),
                         reads=[t_raw[b], t_cw], writes=[t_acc[b]])
                    S.op("dve", lambda e, o=a[:, s0:s1 - 1], i=r[:, s0 + 1:s1]: e.scalar_tensor_tensor(out=o, in0=i, scalar=convw[:, 2, blk:blk + 1], in1=o, op0=ALU.mult, op1=ALU.add),
                         reads=[t_raw[b], t_cw], writes=[t_acc[b]])
                S.op("dve", lambda e, o=a[:, 256:1024:256], i=r[:, 255:1023:256]: e.scalar_tensor_tensor(out=o, in0=i, scalar=nw0[:, blk:blk + 1], in1=o, op0=ALU.mult, op1=ALU.add),
                     reads=[t_raw[b], t_cw], writes=[t_acc[b]])
                S.op("dve", lambda e, o=a[:, 255:1023:256], i=r[:, 256:1024:256]: e.scalar_tensor_tensor(out=o, in0=i, scalar=nw2[:, blk:blk + 1], in1=o, op0=ALU.mult, op1=ALU.add),
                     reads=[t_raw[b], t_cw], writes=[t_acc[b]])
                S.op("act", lambda e, o=xbcT[:, blk, :], i=a: e.activation(out=o, in_=i, func=AF.Silu), reads=[t_acc[b]], writes=[t_xbc[blk]])

            p1(lambda cc: w_in[l][:, OFF_XBC + cc * 128:OFF_XBC + (cc + 1) * 128], 8, 16,
               lambda kc, t0, n: hT[:, kc, t0:t0 + n], lambda kc: [t_hT[kc]], ev_x)
            for t in range(NT):
                bz = nextbank()
                for kc in range(16):
                    mm(banks[bz][:, :], hT[:, kc, t * 128:(t + 1) * 128], wz[:, kc, 0:512], kc == 0, kc == 15, [t_hT[kc], t_wz[0], t_wz[1]], ptok[bz], ptok[bz])
                bd = nextbank()
                for kc in range(16):
                    mm(banks[bd][:, 0:16], hT[:, kc, t * 128:(t + 1) * 128], wz[:, kc, 512:528], kc == 0, kc == 15, [t_hT[kc], t_wz[2]], ptok[bd], ptok[bd])
                S.op("act", lambda e, o=z_tok[:, t, :], i=banks[bz][:, :]: e.activation(out=o, in_=i, func=AF.Silu), reads=[ptok[bz]], writes=[t_z[t]])
                S.op("dve", lambda e, o=dtv_all[:, t, :], i=banks[bd][:, 0:16]: e.tensor_tensor(out=o, in0=i, in1=dtb, op=ALU.add), reads=[ptok[bd], t_dtp], writes=[t_dt[t]])
                S.op("act", lambda e, o=dtv_all[:, t, :]: e.activation(out=o, in_=o, func=AF.Exp), writes=[t_dt[t]])
                S.op("act", lambda e, o=dtv_all[:, t, :]: e.activation(out=o, in_=o, func=AF.Ln, bias=1.0), writes=[t_dt[t]])
                S.op("dve", lambda e, o=dtA_all[:, t, :], i=dtv_all[:, t, :]: e.tensor_tensor(out=o, in0=i, in1=Abc, op=ALU.mult), reads=[t_dtp], writes=[t_dt[t]])
            tap("xbcT", xbcT, t_xbc)
            tap("dtv", small[:, 256:416], t_dt)
            S.flush()
            cs_b = A32(120, 10 * 512).rearrange("p (t d) -> p t d", t=10)
            t_csb = [Tok() for _ in range(10)]
            y_part = A16(140, 10 * 512).rearrange("p (t d) -> p t d", t=10)
            t_yp = [Tok() for _ in range(10)]
            ltm = A32(100, 16 * 128).rearrange("p (h j) -> p h j", h=16)
            MTb = [A16(108 + 4 * i, 16 * 128).rearrange("p (h j) -> p h j", h=16) for i in range(2)]
            cbm = A16(116, 512).rearrange("p (d g i) -> p d g i", d=2, g=2)
            xsb = [A16(117 + i, 512) for i in range(2)]
            Btb = [A16(119 + 0.5 * i, 256) for i in range(2)]
            dtxb = [A16(150, 1024).rearrange("p (d n) -> p d n", d=2), A16(156, 1024).rearrange("p (d n) -> p d n", d=2)]
            Wdb = [A16(152, 1024).rearrange("p (d n) -> p d n", d=2), A16(158, 1024).rearrange("p (d n) -> p d n", d=2)]
            ytmp = A32(154, 512)
            Sst = A32(72, 512)
            Sbf = A16(74, 512)
            Sn = A32(75, 512)
            stf = A32(77, 512).rearrange("p (j n) -> p j n", j=4)
            hl = A32(158, 512).rearrange("p (j n) -> p j n", j=4)
            t_lt, t_cbm, t_y1, t_wg = Tok(), Tok(), Tok(), [Tok(), Tok()]
            t_MT, t_xs, t_Bt, t_dtx, t_W = ([Tok(), Tok()] for _ in range(5))
            t_S, t_Sbf, t_Sn, t_stf = [Tok() for _ in range(4)]
            t_hl = t_W[1]
            wgtb = [small[:, 232:248], small[:, 1180:1196]]
            v8 = lambda ap: ap.rearrange("p (h d) -> p h d", h=8)

            def load_state(dirn, dst, dst_tok):
                S.dma("sp", hl, h0[l, dirn].rearrange("(j p) n -> p j n", p=128), writes=[t_hl])
                bk = nextbank()
                for j in range(4):
                    tr(banks[bk][:, j * 128:(j + 1) * 128], hl[:, j, :], ident_f, [t_hl], ptok[bk], ptok[bk], j == 3, j == 0)
                S.op("dve", lambda e, o=dst, i=banks[bk][:, :]: e.tensor_copy(out=o, in_=i), reads=[ptok[bk]], writes=[dst_tok])

            def store_state(src, src_tok, seq, dirn, bk):
                for j in range(4):
                    tr(banks[bk][:, j * 128:(j + 1) * 128], src[:, j * 128:(j + 1) * 128], ident_f, [src_tok], ptok[bk], ptok[bk], j == 3, j == 0)
                S.op("act", lambda e, o=stf.rearrange("p j n -> p (j n)"), i=banks[bk][:, :]: e.copy(out=o, in_=i), reads=[ptok[bk]], writes=[t_stf])
                S.dma("sp", ns_o[l, seq, dirn].rearrange("(j p) n -> p j n", p=128), stf, reads=[t_stf])

            load_state(0, Sst, t_S)
            S.op("act", lambda e: e.copy(out=Sbf, in_=Sst), reads=[t_S], writes=[t_Sbf])

            def m2_front(c):
                pb_ = c % 2
                MT, xs_tok, B_tok, dtx, Wd, wgt = MTb[pb_], xsb[pb_], Btb[pb_], dtxb[pb_], Wdb[pb_], wgtb[pb_]
                cols = slice(c * 128, (c + 1) * 128)
                b0 = 0
                for j in range(6):
                    tr(bankbf(b0)[:, j * 128:(j + 1) * 128], xbcT[:, j, cols], identb, [t_xbc[j]], ptok[b0], ptok[b0], j == 5, j == 0)
                S.op("act", lambda e, i=bankbf(b0)[:, 0:512]: e.copy(out=xs_tok, in_=i), reads=[ptok[b0]], writes=[t_xs[pb_]])
                S.op("act", lambda e, i=bankbf(b0)[:, 512:768]: e.copy(out=B_tok, in_=i), reads=[ptok[b0]], writes=[t_Bt[pb_]])
                b1 = 1
                dA = dtA_all[:, c, :]
                specs = ((0, 0, 0, 8), (8, 2, 8, 16), (16, 1, 0, 8), (24, 3, 8, 16))
                for si, (o0, ti, a0, a1) in enumerate(specs):
                    mm(banks[b1][:, o0:o0 + 8], tri[:, ti, :], dA[:, a0:a1], True, True, [t_dt[c], t_const], None, ptok[b1] if si == 0 else None)
                mm(banks[b1][:, 32:48], ones_f, dA, True, True, [t_dt[c], t_const], ptok[b1], None)
                S.op("act", lambda e, o=E_all[:, c, :], i=banks[b1][:, 0:48]: e.activation(out=o, in_=i, func=AF.Exp), reads=[ptok[b1]], writes=[t_E[c]])
                for g in range(2):
                    mm(banks[b1][:, 128 + g * 128:256 + g * 128], xbcT[:, 4 + g, cols], xbcT[:, 6 + g, cols], True, True, [t_xbc[4 + g], t_xbc[6 + g]],
                       ptok[b1] if g == 1 else None, ptok[b1] if g == 0 else None)
                for d in range(2):
                    S.op("dve", lambda e, o=cbm[:, d, :, :], i=banks[b1][:, 128:384].rearrange("p (g i) -> p g i", g=2), m=bc_in(tribf[:, d, :], 2):
                         e.tensor_tensor(out=o, in0=i, in1=m, op=ALU.mult), reads=[ptok[b1], t_const], writes=[t_cbm])
                for d in range(2):
                    S.op("pool", lambda e, o=ltm[:, d * 8:d * 8 + 8, :], m=bc_in(tri[:, 1 if d == 0 else 3, :], 8), d_=bc_mid(dA[:, d * 8:d * 8 + 8], 128): e.tensor_tensor(out=o, in0=m, in1=d_, op=ALU.mult),
                         reads=[t_dt[c], t_const], writes=[t_lt])
                    for hh in range(2):
                        bk = 2 + hh
                        for h4 in range(4):
                            hd = d * 8 + hh * 4 + h4
                            mm(banks[bk][:, h4 * 128:(h4 + 1) * 128], ltm[:, hd, :], tri[:, 0 if d == 0 else 2, :], True, True, [t_lt, t_const],
                               ptok[bk] if h4 == 3 else None, ptok[bk] if h4 == 0 else None)
                        h0_ = d * 8 + hh * 4
                        S.op("act", lambda e, o=MT[:, h0_:h0_ + 4, :].rearrange("p h j -> p (h j)"), i=banks[bk][:, :]: e.activation(out=o, in_=i, func=AF.Exp),
                             reads=[ptok[bk]], writes=[t_MT[pb_]])
                        g = hh
                        S.op("dve", lambda e, o=MT[:, h0_:h0_ + 4, :], m=bc_in(cbm[:, d, g, :], 4): e.tensor_tensor(out=o, in0=o, in1=m, op=ALU.mult),
                             reads=[t_cbm], writes=[t_MT[pb_]])
                S.op("dve", lambda e, o=wgt, i=dtv_all[:, c, :], ee=E_all[:, c, 16:32]: e.tensor_tensor(out=o, in0=i, in1=ee, op=ALU.mult), reads=[t_dt[c], t_E[c]], writes=[t_wg[pb_]])
                for d in range(2):
                    S.op("pool", lambda e, o=v8(dtx[:, d, :]), i=v8(xs_tok), s_=bc_mid(dtv_all[:, c, d * 8:d * 8 + 8], 64): e.tensor_tensor(out=o, in0=i, in1=s_, op=ALU.mult),
                         reads=[t_xs[pb_], t_dt[c]], writes=[t_dtx[pb_]])
                    S.op("pool", lambda e, o=v8(Wd[:, d, :]), i=v8(xs_tok), s_=bc_mid(wgt[:, d * 8:d * 8 + 8], 64): e.tensor_tensor(out=o, in0=i, in1=s_, op=ALU.mult),
                         reads=[t_xs[pb_], t_wg[pb_]], writes=[t_W[pb_]])
                S.op("dve", lambda e, o=v8(y_part[:, c, :]), i=v8(xs_tok), s_=bc_mid(dsk, 64): e.tensor_tensor(out=o, in0=i, in1=s_, op=ALU.mult), reads=[t_xs[pb_], t_dtp], writes=[t_yp[c]])

            def m2_back(c):
                pb_ = c % 2
                MT, B_tok, dtx, Wd = MTb[pb_], Btb[pb_], dtxb[pb_], Wdb[pb_]
                cols = slice(c * 128, (c + 1) * 128)
                for d in range(2):
                    bk = 6 + d
                    for h in range(8):
                        mm(banks[bk][:, h * 64:(h + 1) * 64], MT[:, d * 8 + h, :], dtx[:, d, h * 64:(h + 1) * 64], True, True, [t_MT[pb_], t_dtx[pb_]],
                           ptok[bk] if h == 7 else None, ptok[bk] if h == 0 else None)
                for d in range(2):
                    bk = 4 + d
                    for g in range(2):
                        mm(banks[bk][:, g * 256:(g + 1) * 256], B_tok[:, g * 128:(g + 1) * 128], Wd[:, d, g * 256:(g + 1) * 256], True, True, [t_Bt[pb_], t_W[pb_]],
                           ptok[bk] if g == 1 else None, ptok[bk] if g == 0 else None)
                byo = 0
                for g in range(2):
                    mm(banks[byo][:, g * 256:(g + 1) * 256], xbcT[:, 6 + g, cols], Sbf[:, g * 256:(g + 1) * 256], True, True, [t_xbc[6 + g], t_Sbf],
                       ptok[byo] if g == 1 else None, ptok[byo] if g == 0 else None)
                S.op("dve", lambda e, o=v8(Sn), i=v8(Sst), s_=bc_mid(E_all[:, c, 32:40], 64): e.tensor_tensor(out=o, in0=i, in1=s_, op=ALU.mult), reads=[t_S, t_E[c]], writes=[t_Sn])
                S.op("dve", lambda e, i=banks[4][:, :]: e.tensor_tensor(out=Sn, in0=Sn, in1=i, op=ALU.add), reads=[ptok[4]], writes=[t_Sn])
                if c < NT - 1:
                    if c + 1 == 8:
                        S.op("dve", lambda e: e.memset(Sst, 0.0), writes=[t_S])
                    elif (c + 1) % 2 == 0:
                        S.op("dve", lambda e: e.tensor_scalar_mul(out=Sst, in0=Sn, scalar1=samp), reads=[t_Sn, t_const], writes=[t_S])
                    else:
                        S.op("dve", lambda e: e.tensor_copy(out=Sst, in_=Sn), reads=[t_Sn], writes=[t_S])
                    S.op("act", lambda e: e.copy(out=Sbf, in_=Sst), reads=[t_S], writes=[t_Sbf])
                S.op("dve", lambda e, o=v8(ytmp), i=v8(banks[byo][:, :]), s_=bc_mid(E_all[:, c, 0:8], 64): e.tensor_tensor(out=o, in0=i, in1=s_, op=ALU.mult),
                     reads=[ptok[byo], t_E[c]], writes=[t_y1])
                S.op("dve", lambda e, i=banks[6][:, :]: e.tensor_tensor(out=ytmp, in0=ytmp, in1=i, op=ALU.add), reads=[ptok[6]], writes=[t_y1])
                S.op("dve", lambda e, i=banks[7][:, :]: e.tensor_tensor(out=ytmp, in0=ytmp, in1=i, op=ALU.add), reads=[ptok[7]], writes=[t_y1])
                S.op("dve", lambda e, o=y_part[:, c, :]: e.tensor_tensor(out=o, in0=ytmp, in1=o, op=ALU.add), reads=[t_y1], writes=[t_yp[c]])
                S.op("act", lambda e, o=cs_b[:, c, :], i=banks[5][:, :]: e.copy(out=o, in_=i), reads=[ptok[5]], writes=[t_csb[c]])
                if c % 2 == 1:
                    store_state(Sn, t_Sn, c // 2, 0, byo)

            m2_front(0)
            for c in range(NT):
                if c + 1 < NT:
                    m2_front(c + 1)
                m2_back(c)
            S.flush()
            ysets = [dict(ytmp=A32(154, 512), yz=A32(150, 512), y2=A32(156, 512), otk=A16(117, 512), ssq=small[:, 248:249]),
                     dict(ytmp=A32(110, 512), yz=A32(112, 512), y2=A32(114, 512), otk=A16(116, 512), ssq=small[:, 249:250])]
            ytoks = [dict(y3=Tok(), yz=Tok(), y2=Tok(), otk=Tok(), ssq=Tok()) for _ in range(2)]
            S.op("dve", lambda e: e.memset(Sst, 0.0), writes=[t_S])
            S.op("dve", lambda e: e.memset(Sbf, 0.0), writes=[t_Sbf])
            ybank = {}

            def m3_state(c):
                cols = slice(c * 128, (c + 1) * 128)
                bk = nextbank()
                ybank[c] = bk
                for g in range(2):
                    mm(banks[bk][:, g * 256:(g + 1) * 256], xbcT[:, 6 + g, cols], Sbf[:, g * 256:(g + 1) * 256], True, True, [t_xbc[6 + g], t_Sbf],
                       ptok[bk] if g == 1 else None, ptok[bk] if g == 0 else None)
                S.op("dve", lambda e, o=v8(Sn), i=v8(Sst), s_=bc_mid(E_all[:, c, 40:48], 64): e.tensor_tensor(out=o, in0=i, in1=s_, op=ALU.mult), reads=[t_S, t_E[c]], writes=[t_Sn])
                S.op("dve", lambda e, i=cs_b[:, c, :]: e.tensor_tensor(out=Sn, in0=Sn, in1=i, op=ALU.add), reads=[t_csb[c]], writes=[t_Sn])
                if c % 2 == 0:
                    store_state(Sn, t_Sn, c // 2, 1, nextbank())
                if c > 0:
                    if c == 8:
                        load_state(1, Sst, t_S)
                    elif c % 2 == 0:
                        S.op("dve", lambda e: e.tensor_scalar_mul(out=Sst, in0=Sn, scalar1=samp), reads=[t_Sn, t_const], writes=[t_S])
                    else:
                        S.op("dve", lambda e: e.tensor_copy(out=Sst, in_=Sn), reads=[t_Sn], writes=[t_S])
                    S.op("act", lambda e: e.copy(out=Sbf, in_=Sst), reads=[t_S], writes=[t_Sbf])

            def m3_y(c, b):
                cols = slice(c * 128, (c + 1) * 128)
                bk = ybank[c]
                B_, K_ = ysets[b], ytoks[b]
                ytmp, yz, y2, otk, ssq = B_["ytmp"], B_["yz"], B_["y2"], B_["otk"], B_["ssq"]
                S.op("dve", lambda e, o=v8(ytmp), i=v8(banks[bk][:, :]), s_=bc_mid(E_all[:, c, 8:16], 64): e.tensor_tensor(out=o, in0=i, in1=s_, op=ALU.mult),
                     reads=[ptok[bk], t_E[c]], writes=[K_["y3"]])
                S.op("dve", lambda e, o=ytmp, i=y_part[:, c, :]: e.tensor_tensor(out=o, in0=o, in1=i, op=ALU.add), reads=[t_yp[c]], writes=[K_["y3"]])
                S.op("pool", lambda e, o=yz, a_=ytmp, i=z_tok[:, c, :]: e.tensor_tensor(out=o, in0=a_, in1=i, op=ALU.mult), reads=[K_["y3"], t_z[c]], writes=[K_["yz"]])
                S.op("act", lambda e, o=y2, i=yz: e.activation(out=o, in_=i, func=AF.Square), reads=[K_["yz"]], writes=[K_["y2"]])
                S.op("dve", lambda e, o=ssq, i=y2: e.reduce_sum(out=o, in_=i, axis=AX.X), reads=[K_["y2"]], writes=[K_["ssq"]])
                S.op("act", lambda e, o=ssq: e.activation(out=o, in_=o, func=AF.Sqrt, scale=1.0 / 512, bias=epsc), reads=[t_const], writes=[K_["ssq"]])
                S.op("dve", lambda e, o=ssq: e.reciprocal(out=o, in_=o), writes=[K_["ssq"]])
                S.op("dve", lambda e, o=otk, i=yz, sc=ssq: e.scalar_tensor_tensor(out=o, in0=i, scalar=sc, in1=gssm, op0=ALU.mult, op1=ALU.mult), reads=[K_["yz"], K_["ssq"], t_gs], writes=[K_["otk"]])
                bt = nextbank()
                for j in range(4):
                    tr(bankbf(bt)[:, j * 128:(j + 1) * 128], otk[:, j * 128:(j + 1) * 128], identb, [K_["otk"]], ptok[bt], ptok[bt], j == 3, j == 0)
                S.op("act", lambda e, o=oT[:, 8:12, cols], i=bankbf(bt)[:, 0:512].rearrange("p (j c) -> p j c", j=4): e.copy(out=o, in_=i), reads=[ptok[bt]], writes=t_oT[8:12])

            chunks = list(range(NT - 1, -1, -1))
            for idx, c in enumerate(chunks):
                m3_state(c)
                if idx >= 1:
                    m3_y(chunks[idx - 1], (idx - 1) % 2)
            m3_y(chunks[-1], (NT - 1) % 2)
            S.flush()

        def MIXERS(l):
            if "attn" in skip:
                for fc in range(0, 4):
                    S.op("dve", lambda e, o=oT[:, fc, :]: e.memset(o, 0.0), writes=[t_oT[fc]])
            else:
                stage_attn(l)
            if "fnet" in skip:
                for fc in range(4, 8):
                    S.op("dve", lambda e, o=oT[:, fc, :]: e.memset(o, 0.0), writes=[t_oT[fc]])
            else:
                stage_fourier(l)
            if "ssm" in skip:
                for fc in range(8, 12):
                    S.op("dve", lambda e, o=oT[:, fc, :]: e.memset(o, 0.0), writes=[t_oT[fc]])
            else:
                stage_ssm(l)
            if "gmlp" in skip:
                for fc in range(12, 16):
                    S.op("dve", lambda e, o=oT[:, fc, :]: e.memset(o, 0.0), writes=[t_oT[fc]])
            else:
                stage_gmlp(l)
            for nm in ("att", "fnet", "ssm", "gmlp"):
                i0 = {"att": 0, "fnet": 4, "ssm": 8, "gmlp": 12}[nm]
                tap("o_" + nm, oT[:, i0:i0 + 4, :], t_oT[i0:i0 + 4])
            S.flush()

        order = ["stage0", "mod", "norm1", "mixers", "merge", "wout", "norm2", "ffn"]
        lim = order.index(upto) if upto else 99
        stage0()
        for l in range(nlayers):
            if l == 0:
                mod_drain(2)
                S.flush()
            if l + 1 < nlayers:
                mod_q.extend((l + 1, j) for j in range(96))
            if lim >= 2:
                stage_norm(l, 0)
                spill_x()
                S.flush()
            if lim >= 3:
                MIXERS(l)
            if lim >= 4:
                stage_merge(l)
            if lim >= 5:
                stage_wffn(l, lim)
            mod_drain(len(mod_q) if l + 1 < nlayers else 0)
            S.flush()

        ytok = [A32(0 + 8 * i, 2048) for i in range(2)]
        t_ytok = [Tok(), Tok()]
        for t in range(NT):
            b = t % 2
            for q4 in range(4):
                bk = nextbank()
                for j in range(4):
                    fc = q4 * 4 + j
                    tr(banks[bk][:, j * 128:(j + 1) * 128], xT[:, fc, t * 128:(t + 1) * 128], ident_f,
                       [t_xT[fc]], ptok[bk], ptok[bk], j == 3, j == 0)
                copy_any(alt(), ytok[b][:, q4 * 512:(q4 + 1) * 512], banks[bk][:, :], [ptok[bk]], [t_ytok[b]])
            S.dma("sp", y_o[t * 128:(t + 1) * 128, :], ytok[b], reads=[t_ytok[b]])
        S.flush()
    return nc


_NC_CACHE = {}


def _tables():
    half = 16
    freqs = 10000.0 ** (-np.arange(half, dtype=np.float64) / half)
    l = np.arange(1024)
    rows = (l // 64).astype(np.float64)
    cols = (l % 64).astype(np.float64)
    ar = rows[:, None] * freqs[None, :]
    ac = cols[:, None] * freqs[None, :]
    cos_s = np.concatenate([np.cos(ar), np.cos(ar), np.cos(ac), np.cos(ac)], axis=1)
    sin_s = np.concatenate([-np.sin(ar), np.sin(ar), -np.sin(ac), np.sin(ac)], axis=1)
    cos_p = np.ones((1024, 64))
    sin_p = np.zeros((1024, 64))
    cosB = np.ones((256, 64))
    sinB = np.zeros((256, 64))
    rope_s = (np.concatenate([cos_s, cosB]).astype(np.float32), np.concatenate([sin_s, sinB]).astype(np.float32))
    rope_p = (np.concatenate([cos_p, cosB]).astype(np.float32), np.concatenate([sin_p, sinB]).astype(np.float32))

    def dft(L):
        a = 2.0 * np.pi * np.outer(np.arange(L), np.arange(L)) / L
        s = 1.0 / np.sqrt(L * 128.0)
        return np.cos(a) * s, -np.sin(a) * s

    c1024, s1024 = dft(1024)
    c256, s256 = dft(256)
    dftA_s = np.stack([c1024, s1024]).astype(np.float32)
    cb = np.zeros((1024, 1024))
    sb = np.zeros((1024, 1024))
    for i in range(4):
        cb[i * 256:(i + 1) * 256, i * 256:(i + 1) * 256] = c256
        sb[i * 256:(i + 1) * 256, i * 256:(i + 1) * 256] = s256
    dftA_p = np.stack([cb, sb]).astype(np.float32)
    dftB = np.stack([c256, s256]).astype(np.float32)
    a = 2.0 * np.pi * np.outer(np.arange(128), np.arange(128)) / 128.0
    dftC = np.concatenate([np.cos(a), np.sin(a)], axis=1).astype(np.float32)
    BIG = 30000.0
    km_s = np.zeros((4, 1536), np.float32)
    qm_s = np.zeros((4, T), np.float32)
    km_p = np.zeros((4, 1536), np.float32)
    qm_p = np.zeros((4, T), np.float32)
    km_p[:, 0:256] = 1.0
    for s in range(4):
        km_p[s, 256 + s * 256:256 + (s + 1) * 256] = 1.0
        qm_p[:, s * 256:(s + 1) * 256] = -BIG
        qm_p[s, s * 256:(s + 1) * 256] = 0.0
    k = np.arange(128)[:, None]
    i = np.arange(128)[None, :]
    tri = np.stack([(k <= i), (k > i), (k >= i), (k < i)]).astype(np.float32)
    ident = np.eye(128, dtype=np.float32)
    return dict(rope_s=rope_s, rope_p=rope_p, dftA_s=dftA_s, dftA_p=dftA_p, dftB=dftB, dftC=dftC,
                km_s=km_s, qm_s=qm_s, km_p=km_p, qm_p=qm_p, tri=tri, ident=ident)


def _prompt_ids(core):
    if core < 2:
        return [], core
    base = 2 + (core - 2) * 5
    return [base, base + 1, base + 2, base + 3], base + 4


def make_in_maps(inp):
    tb = _tables()
    f = lambda a: np.ascontiguousarray(np.asarray(a, dtype=np.float32))
    x_prompt = f(inp["x_prompt"])
    x_sample = f(inp["x_sample"])
    c = f(inp["c"])
    c_ctx = f(inp["c_ctx"])
    cache_k = f(inp["cache_k"])
    cache_v = f(inp["cache_v"])
    state = f(inp["state_ssm"])
    shared = {
        "dftB": tb["dftB"], "dftC": tb["dftC"], "ident": tb["ident"], "tri": tb["tri"],
        "w_mod": f(inp["w_mod"]), "b_mod": f(inp["b_mod"]), "w_in": f(inp["w_in"]),
        "q_norm_g": f(inp["q_norm_g"]), "k_norm_g": f(inp["k_norm_g"]),
        "conv_w": f(inp["conv_w"]), "conv_b": f(inp["conv_b"]),
        "a_log": f(inp["a_log"]).reshape(2, 16), "dt_bias": f(inp["dt_bias"]).reshape(2, 16),
        "d_skip": f(inp["d_skip"]), "ssm_norm_g": f(inp["ssm_norm_g"]), "gmlp_norm_g": f(inp["gmlp_norm_g"]),
        "w_spatial": f(inp["w_spatial"]), "b_spatial": f(inp["b_spatial"]).reshape(2, 512),
        "w_branch": f(inp["w_branch"]), "w_out": f(inp["w_out"]), "w_ff1": f(inp["w_ff1"]), "w_ff2": f(inp["w_ff2"]),
    }
    maps = []
    for core in range(8):
        a_ids, b_id = _prompt_ids(core)
        m = dict(shared)
        if core < 2:
            xa = x_sample[core]
            conds = np.stack([c[core], c_ctx])
            m["ctxk"] = np.ascontiguousarray(cache_k[core].reshape(2, 256, 128))
            m["ctxv"] = np.ascontiguousarray(cache_v[core].reshape(2, 256, 128))
            m["h0"] = np.ascontiguousarray(state[core].reshape(2, 2, 512, 128))
            m["flags"] = np.tile(np.array([[1.0, 0.0, 0.0, 0.0]], np.float32), (128, 1))
            m["ropec"], m["ropes"] = tb["rope_s"]
            m["kmask"], m["qmask"] = tb["km_s"], tb["qm_s"]
            m["dftA"] = tb["dftA_s"]
        else:
            xa = x_prompt[a_ids].reshape(1024, D)
            conds = np.stack([c_ctx, c_ctx])
            m["ctxk"] = np.zeros((2, 256, 128), np.float32)
            m["ctxv"] = np.zeros((2, 256, 128), np.float32)
            m["h0"] = np.zeros((2, 2, 512, 128), np.float32)
            m["flags"] = np.tile(np.array([[0.0, 1.0, 0.0, 0.0]], np.float32), (128, 1))
            m["ropec"], m["ropes"] = tb["rope_p"]
            m["kmask"], m["qmask"] = tb["km_p"], tb["qm_p"]
            m["dftA"] = tb["dftA_p"]
        m["xin"] = np.ascontiguousarray(np.concatenate([xa, x_prompt[b_id]], axis=0))
        m["cond"] = np.ascontiguousarray(conds)
        maps.append(m)
    return maps


def assemble(results):
    yp = np.zeros((32, 256, D), np.float32)
    ys = np.zeros((2, 1024, D), np.float32)
    nk = np.zeros((32, 2, 256, 2, 64), np.float32)
    nv = np.zeros((32, 2, 256, 2, 64), np.float32)
    ns = np.zeros((32, 2, 2, 8, 64, 128), np.float32)
    for core in range(8):
        r = results[core]
        a_ids, b_id = _prompt_ids(core)
        y = np.asarray(r["y"])
        k_ = np.asarray(r["nk"])
        v_ = np.asarray(r["nv"])
        s_ = np.asarray(r["ns"])
        if core < 2:
            ys[core] = y[0:1024]
        seqs = [(pid, i * 256, i) for i, pid in enumerate(a_ids)] + [(b_id, 1024, 4)]
        for pid, r0, si in seqs:
            yp[pid] = y[r0:r0 + 256]
            nk[pid] = k_[:, r0:r0 + 256, :].reshape(2, 256, 2, 64)
            nv[pid] = v_[:, r0:r0 + 256, :].reshape(2, 256, 2, 64)
            ns[pid] = s_[:, si].reshape(2, 2, 8, 64, 128)
    return yp, ys, nk, nv, ns


def kernel(**inputs):
    if "nc" not in _NC_CACHE:
        _NC_CACHE["nc"] = build()
    nc = _NC_CACHE["nc"]
    in_maps = make_in_maps(inputs)
    res = run_bass_kernel_spmd(nc, in_maps, core_ids=list(range(8)))
    return assemble(res.results)
```

```python
import numpy as np
from contextlib import ExitStack
import concourse.bass as bass
import concourse.mybir as mybir
from concourse.bass_utils import run_bass_kernel_spmd

F32 = mybir.dt.float32
BF16 = mybir.dt.bfloat16
AF = mybir.ActivationFunctionType
ALU = mybir.AluOpType
AX = mybir.AxisListType

D = 2048
T = 1280
NT = 10
TGS = [(0, 512), (512, 512), (1024, 256)]
NIN = 12048
DFF = 8192
EPS = 1e-6
OFF_Q, OFF_K, OFF_V, OFF_F, OFF_XBC, OFF_Z, OFF_DT, OFF_UV, OFF_G = 0, 512, 640, 768, 1280, 2304, 2816, 2832, 3856
NDS = 8
ENGS = ("pe", "act", "dve", "pool", "sp")


class Tok:
    __slots__ = ("w", "rs", "ep")

    def __init__(self):
        self.w = None
        self.rs = {}
        self.ep = -1


class Sched:
    def __init__(self, nc, es):
        self.nc = nc
        self.sem = {e: es.enter_context(nc.semaphore("sem_" + e)) for e in ENGS}
        self.cnt = {e: 0 for e in ENGS}
        self.dsem = {q: [es.enter_context(nc.semaphore("d%s%d" % (q, i))) for i in range(NDS)] for q in ("pool", "sp")}
        self.dcnt = {q: [0] * NDS for q in ("pool", "sp")}
        self.dnext = {"pool": 0, "sp": 0}
        self.ops = {e: [] for e in ENGS}
        self.waited = {e: {} for e in ENGS}
        self.inflight = {"pool": [], "sp": []}
        self.epoch = 0
        self.nblk = 0

    def _sync(self, t):
        if t.ep != self.epoch:
            t.w = None
            t.rs = {}
            t.ep = self.epoch

    def _need(self, eng, ev, waits):
        if ev is None:
            return
        key, sem, val = ev
        if key == "pe" and eng == "pe":
            return
        if self.waited[eng].get(key, 0) >= val:
            return
        if key in self.cnt:
            assert val <= self.cnt[key], "wait on a not-yet-emitted milestone (deadlock hazard): %s waits %s>=%d" % (eng, key, val)
        self.waited[eng][key] = val
        waits.append((sem, val))

    def _deps(self, eng, reads, writes, guard=()):
        waits = []
        for t in reads:
            self._sync(t)
            self._need(eng, t.w, waits)
        for t in list(writes) + list(guard):
            self._sync(t)
            self._need(eng, t.w, waits)
            for ev in t.rs.values():
                self._need(eng, ev, waits)
        return waits

    def op(self, eng, fn, reads=(), writes=(), sig=True, guard=()):
        waits = self._deps(eng, reads, writes, guard)
        if sig:
            self.cnt[eng] += 1
            ev = (eng, self.sem[eng], self.cnt[eng])
            for t in writes:
                t.w = ev
                t.rs = {}
            for t in reads:
                if t not in writes:
                    t.rs[eng] = ev
        else:
            assert not writes
            ev = (eng, self.sem[eng], self.cnt[eng] + 1)
            for t in reads:
                t.rs[eng] = ev
        self.ops[eng].append((waits, fn, "inc" if sig else None, None))

    def dma(self, q, out, in_, reads=(), writes=(), **kw):
        waits = self._deps(q, reads, writes)
        i = self.dnext[q]
        self.dnext[q] = (i + 1) % NDS
        key = "d%s%d" % (q, i)
        sem = self.dsem[q][i]
        if self.dcnt[q][i] > 0:
            self._need(q, (key, sem, 16 * self.dcnt[q][i]), waits)
        self.dcnt[q][i] += 1
        ev = (key, sem, 16 * self.dcnt[q][i])
        for t in writes:
            t.w = ev
            t.rs = {}
        for t in reads:
            t.rs[key] = ev
        self.inflight[q].append(ev)
        self.ops[q].append((waits, (lambda e, o=out, a=in_, k=kw: e.dma_start(out=o, in_=a, **k)), "dma", sem))

    def flush(self):
        nc = self.nc
        for q in ("pool", "sp"):
            waits = []
            for ev in self.inflight[q]:
                self._need(q, ev, waits)
            self.inflight[q] = []
            if waits:
                self.ops[q].append((waits, None, None, None))
        ops = self.ops
        self.ops = {e: [] for e in ENGS}
        semmap = self.sem

        def body(eng, lst):
            def f(e):
                for waits, fn, kind, extra in lst:
                    for sem, val in waits:
                        e.wait_ge(sem, val)
                    if fn is None:
                        continue
                    ins = fn(e)
                    if kind == "inc":
                        ins.then_inc(semmap[eng], 1)
                    elif kind == "dma":
                        ins.then_inc(extra, 16)
            return f

        self.nblk += 1
        with nc.Block() as block:
            if ops["pe"]:
                block.tensor(body("pe", ops["pe"]))
            if ops["act"]:
                block.scalar(body("act", ops["act"]))
            if ops["dve"]:
                block.vector(body("dve", ops["dve"]))
            if ops["pool"]:
                block.gpsimd(body("pool", ops["pool"]))
            if ops["sp"]:
                block.sync(body("sp", ops["sp"]))
        self.epoch += 1


def bc_mid(ap2, n):
    return ap2.unsqueeze(2).to_broadcast([ap2.shape[0], ap2.shape[1], n])


def bc_in(ap2, n):
    return ap2.unsqueeze(1).to_broadcast([ap2.shape[0], n, ap2.shape[1]])


def build(nlayers=2, taps=(), skip=(), upto=None, NL=2, ndbg=9):
    nc = bass.Bass("TRN2", target_bir_lowering=False)

    def din(name, shape, dt=F32):
        return nc.dram_tensor(name, list(shape), dt, kind="ExternalInput").ap()

    def dout(name, shape, dt=F32):
        return nc.dram_tensor(name, list(shape), dt, kind="ExternalOutput").ap()

    xin = din("xin", [T, D])
    cond = din("cond", [2, D])
    ctxk = din("ctxk", [2, 256, 128])
    ctxv = din("ctxv", [2, 256, 128])
    h0 = din("h0", [2, 2, 512, 128])
    flags = din("flags", [128, 4])
    ropec = din("ropec", [T, 64])
    ropes = din("ropes", [T, 64])
    kmask = din("kmask", [4, 1536])
    qmask = din("qmask", [4, T])
    dftA = din("dftA", [2, 1024, 1024])
    dftB = din("dftB", [2, 256, 256])
    dftC = din("dftC", [128, 256])
    identd = din("ident", [128, 128])
    trid = din("tri", [4, 128, 128])
    w_mod = din("w_mod", [NL, D, 6 * D])
    b_mod = din("b_mod", [NL, 6 * D])
    w_in = din("w_in", [NL, D, NIN])
    q_g = din("q_norm_g", [NL, 64])
    k_g = din("k_norm_g", [NL, 64])
    conv_w = din("conv_w", [NL, 3, 1024])
    conv_b = din("conv_b", [NL, 1024])
    a_log = din("a_log", [NL, 16])
    dt_bias = din("dt_bias", [NL, 16])
    d_skip = din("d_skip", [NL, 8])
    ssm_g = din("ssm_norm_g", [NL, 512])
    gmlp_g = din("gmlp_norm_g", [NL, 512])
    w_sp = din("w_spatial", [NL, 4, 128, 128])
    b_sp = din("b_spatial", [NL, 512])
    w_br = din("w_branch", [NL, 4, 512, D])
    w_out = din("w_out", [NL, D, D])
    w_ff1 = din("w_ff1", [NL, D, DFF])
    w_ff2 = din("w_ff2", [NL, DFF, D])

    y_o = dout("y", [T, D])
    nk_o = dout("nk", [2, T, 128])
    nv_o = dout("nv", [2, T, 128])
    ns_o = dout("ns", [2, 5, 2, 512, 128])
    xs = nc.dram_tensor("xs", [16, 128, T], F32, kind="Internal").ap()

    es = ExitStack()
    with es:
        KB = 512
        arena = es.enter_context(nc.sbuf_tensor("arena", [128, 192 * KB], BF16))
        cst = es.enter_context(nc.sbuf_tensor("cst", [128, 3420], F32))
        banks = [es.enter_context(nc.psum_tensor("ps%d" % i, [128, 512], F32)) for i in range(8)]
        ptok = [Tok() for _ in range(8)]
        S = Sched(nc, es)

        def A16(off_kb, n):
            o = int(round(off_kb * KB))
            return arena[:, o:o + n]

        def A32(off_kb, n):
            o = int(round(off_kb * KB))
            return arena[:, o:o + 2 * n].bitcast(F32)

        def bankbf(bk):
            return banks[bk][:, :].bitcast(BF16)

        c_off = [0]

        def calloc(n):
            o = c_off[0]
            c_off[0] += n
            assert c_off[0] <= 3420
            return cst[:, o:o + n]

        ident_f = calloc(128)
        tri = calloc(512).rearrange("p (a b) -> p a b", a=4)
        ones_f = calloc(128)
        flg = calloc(4)
        modT = calloc(2 * 2 * 96).rearrange("p (l s j) -> p l s j", l=2, s=2)
        identb = calloc(64).bitcast(BF16)
        onesb = calloc(64).bitcast(BF16)
        tribf = calloc(128).bitcast(BF16).rearrange("p (a b) -> p a b", a=2)
        epsc = calloc(1)
        small = calloc(1200)
        scT_c = calloc(16).bitcast(BF16).rearrange("p (s k) -> p s k", s=2)
        bmT_c = calloc(192).rearrange("p (l j) -> p l j", l=2)
        t_const = Tok()
        t_modl = [Tok(), Tok()]
        t_modc = Tok()

        tapn = [0]

        def tap(name, ap, toks):
            if name not in taps:
                return
            shp = list(ap.shape)
            d = nc.dram_tensor("dbg_" + name, shp, ap.dtype, kind="ExternalOutput").ap()
            S.dma("sp", d, ap, reads=toks)

        NR = 6
        ring = [A16(160 + 4 * i, 2048) for i in range(NR)]
        rtok = [Tok() for _ in range(NR)]
        rpos = [0]

        def wload(w2d, nk, ncols=128):
            i = rpos[0]
            rpos[0] = (i + 1) % NR
            v = ring[i][:, 0:nk * ncols].rearrange("p (k n) -> p k n", k=nk)
            S.dma("pool", v, w2d.rearrange("(k p) n -> p k n", p=128), writes=[rtok[i]])
            return v, rtok[i]

        pb = [0]

        def nextbank(lo=0, hi=7):
            b = lo + (pb[0] % (hi - lo))
            pb[0] += 1
            return b

        ev_alt = [0]

        def alt():
            ev_alt[0] ^= 1
            return "act" if ev_alt[0] else "dve"

        def copy_any(eng, o, i, reads, writes):
            if eng == "act":
                S.op("act", lambda e, o=o, i=i: e.copy(out=o, in_=i), reads=reads, writes=writes)
            else:
                S.op(eng, lambda e, o=o, i=i: e.tensor_copy(out=o, in_=i), reads=reads, writes=writes)

        def mm(o, a, b, st, sp, reads, wtok, gtok, fsig=False):
            S.op("pe", lambda e, o=o, a=a, b=b, st=st, sp=sp: e.matmul(o, lhsT=a, rhs=b, start=st, stop=sp),
                 reads=reads, writes=[wtok] if (sp and wtok is not None) else [],
                 guard=[gtok] if (st and gtok is not None) else [], sig=bool((sp and wtok is not None) or fsig))

        def tr(o, i, idn, reads, wtok, gtok, last, first):
            S.op("pe", lambda e, o=o, i=i, idn=idn: e.transpose(out=o, in_=i, identity=idn),
                 reads=list(reads) + [t_const], writes=[wtok] if last else [], guard=[gtok] if first else [], sig=last)

        def load_T(dst, src2d, R, tmp, tok):
            tt = Tok()
            S.dma("sp", tmp[0:R, :], src2d, writes=[tt])
            bk = nextbank()
            tr(banks[bk][:, 0:R], tmp[0:R, :], ident_f[0:R, 0:R], [tt], ptok[bk], ptok[bk], True, True)
            S.op("dve", lambda e, o=dst, i=banks[bk][:, 0:R]: e.tensor_copy(out=o, in_=i), reads=[ptok[bk]], writes=[tok])

        def p1(wfn, nblk, nk, rhs_fn, rtoks_fn, evac, tgs=TGS, modn=0):
            for blk in range(nblk):
                mod_drain(modn)
                wv, wt = wload(wfn(blk), nk)
                for tgi, (t0, n) in enumerate(tgs):
                    bk = nextbank()
                    for kc in range(nk):
                        mm(banks[bk][:, 0:n], wv[:, kc, :], rhs_fn(kc, t0, n), kc == 0, kc == nk - 1,
                           [wt] + rtoks_fn(kc), ptok[bk], ptok[bk])
                    evac(blk, tgi, t0, n, bk)

        S.dma("sp", ident_f, identd, writes=[t_const])
        S.dma("sp", tri, trid.rearrange("a p b -> p a b"), writes=[t_const])
        S.dma("sp", flg, flags, writes=[t_const])
        S.op("dve", lambda e: e.memset(ones_f, 1.0), writes=[t_const])
        S.op("dve", lambda e: e.memset(epsc, EPS), writes=[t_const])
        S.op("dve", lambda e: e.tensor_copy(out=identb, in_=ident_f), reads=[t_const], writes=[t_const])
        S.op("dve", lambda e: e.memset(onesb, 1.0), writes=[t_const])
        S.op("dve", lambda e: e.tensor_copy(out=tribf[:, 0, :], in_=tri[:, 0, :]), reads=[t_const], writes=[t_const])
        S.op("dve", lambda e: e.tensor_copy(out=tribf[:, 1, :], in_=tri[:, 2, :]), reads=[t_const], writes=[t_const])
        S.flush()

        xT = A32(80, 16 * T).rearrange("p (f t) -> p f t", f=16)
        t_xT = [Tok() for _ in range(16)]
        hT = A16(0, 16 * T).rearrange("p (f t) -> p f t", f=16)
        t_hT = [Tok() for _ in range(16)]
        oT = A16(80, 16 * T).rearrange("p (f t) -> p f t", f=16)
        t_oT = [Tok() for _ in range(16)]
        mgT = A16(40, 16 * T).rearrange("p (f t) -> p f t", f=16)
        t_mg = [Tok() for _ in range(16)]

        def stage0():
            xtok = [A32(0 + 8 * i, 2048) for i in range(2)]
            t_xtok = [Tok(), Tok()]
            mod_setup()
            mod_q.extend((0, j) for j in range(96))
            for t in range(NT):
                b = t % 2
                mod_drain(3)
                S.dma("sp", xtok[b], xin[t * 128:(t + 1) * 128, :], writes=[t_xtok[b]])
                for q4 in range(4):
                    bk = nextbank()
                    for j in range(4):
                        fc = q4 * 4 + j
                        tr(banks[bk][:, j * 128:(j + 1) * 128], xtok[b][:, fc * 128:(fc + 1) * 128], ident_f,
                           [t_xtok[b]], ptok[bk], ptok[bk], j == 3, j == 0)
                    dst = xT[:, q4 * 4:q4 * 4 + 4, t * 128:(t + 1) * 128]
                    src = banks[bk][:, :].rearrange("p (a b) -> p a b", a=4)
                    copy_any(alt(), dst, src, [ptok[bk]], t_xT[q4 * 4:q4 * 4 + 4])
            S.flush()

        def mod_setup():
            condT = A32(40, 32).rearrange("p (s k) -> p s k", s=2)
            load_T(A32(40, 32), cond.rearrange("s (k p) -> (s k) p", p=128), 32, A32(44, 128), t_modc)
            for l_ in range(nlayers):
                load_T(bmT_c[:, l_, :], b_mod[l_].rearrange("(j p) -> j p", p=128), 96, A32(45 + l_, 128), t_modc)
            S.op("act", lambda e: e.activation(out=scT_c, in_=condT, func=AF.Silu), reads=[t_modc], writes=[t_modc])

        mod_q = []

        def mod_block(l, j):
            wv, wt = wload(w_mod[l][:, j * 128:(j + 1) * 128], 16)
            for kc in range(16):
                mm(banks[7][:, 2 * j:2 * j + 2], wv[:, kc, :], scT_c[:, :, kc], kc == 0, kc == 15,
                   [wt, t_modc], ptok[7], ptok[7])
            if j % 16 == 15:
                w = j // 16
                pv = banks[7][:, 0:192].rearrange("p (j s) -> p j s", s=2)
                for s_ in range(2):
                    S.op("dve", lambda e, s_=s_: e.tensor_tensor(out=modT[:, l, s_, 16 * w:16 * w + 16], in0=pv[:, 16 * w:16 * w + 16, s_], in1=bmT_c[:, l, 16 * w:16 * w + 16], op=ALU.add),
                         reads=[ptok[7], t_modc], writes=[t_modl[l]])
                    if w in (1, 4):
                        S.op("dve", lambda e, s_=s_: e.tensor_scalar_add(out=modT[:, l, s_, 16 * w:16 * w + 16], in0=modT[:, l, s_, 16 * w:16 * w + 16], scalar1=1.0),
                             writes=[t_modl[l]])

        def mod_drain(n):
            for _ in range(n):
                if mod_q:
                    l_, j_ = mod_q.pop(0)
                    mod_block(l_, j_)

        def stage_norm(l, which):
            jsh = 0 if which == 0 else 48
            jsc = 16 if which == 0 else 64
            sq = [A16(40 + i, 512) for i in range(4)]
            t_sq = [Tok() for _ in range(4)]
            rstd = A32(44, T)
            t_rstd = Tok()
            tmpf = [A32(50 + 5 * i, T) for i in range(3)]
            t_tmp = [Tok(), Tok(), Tok()]
            k = 0
            for tgi, (t0, n) in enumerate(TGS):
                bk = nextbank()
                for fc in range(16):
                    b = k % 4
                    sqe = ("act", "dve", "act", "pool")[k % 4]
                    k += 1
                    if sqe == "act":
                        S.op("act", lambda e, o=sq[b][:, 0:n], i=xT[:, fc, t0:t0 + n]: e.activation(out=o, in_=i, func=AF.Square),
                             reads=[t_xT[fc]], writes=[t_sq[b]])
                    else:
                        S.op(sqe, lambda e, o=sq[b][:, 0:n], i=xT[:, fc, t0:t0 + n]: e.tensor_tensor(out=o, in0=i, in1=i, op=ALU.mult),
                             reads=[t_xT[fc]], writes=[t_sq[b]])
                    mm(banks[bk][:, 0:n], onesb, sq[b][:, 0:n], fc == 0, fc == 15, [t_sq[b], t_const], ptok[bk], ptok[bk], fsig=True)
                S.op("act", lambda e, o=rstd[:, t0:t0 + n], i=banks[bk][:, 0:n]: e.activation(out=o, in_=i, func=AF.Sqrt, scale=1.0 / D, bias=epsc),
                     reads=[ptok[bk], t_const], writes=[t_rstd])
                S.op("dve", lambda e, o=rstd[:, t0:t0 + n]: e.reciprocal(out=o, in_=o), reads=[t_rstd], writes=[t_rstd])
            if ndbg < 2:
                return
            for fc in range(16):
                b = fc % 3
                S.op("pool" if fc in (1, 4, 7, 10, 13, 15) else "dve", lambda e, o=tmpf[b], i=xT[:, fc, :]: e.tensor_tensor(out=o, in0=i, in1=rstd, op=ALU.mult),
                     reads=[t_xT[fc], t_rstd], writes=[t_tmp[b]])
                if ndbg < 3:
                    continue
                for s, (c0, c1) in enumerate(((0, 1024), (1024, 1280))):
                    S.op("act", lambda e, o=hT[:, fc, c0:c1], i=tmpf[b][:, c0:c1], sc=modT[:, l, s, jsc + fc:jsc + fc + 1], sh=modT[:, l, s, jsh + fc:jsh + fc + 1]:
                         e.activation(out=o, in_=i, func=AF.Identity, scale=sc, bias=sh),
                         reads=[t_tmp[b], t_modl[l]], writes=[t_hT[fc]])

        def spill_x():
            if "spill" in skip:
                return
            for fc in range(16):
                S.dma("sp", xs[fc], xT[:, fc, :], reads=[t_xT[fc]])

        def reload_x(fcs=range(16)):
            for fc in fcs:
                S.dma("sp", xT[:, fc, :], xs[fc], writes=[t_xT[fc]])

        def stage_merge(l):
            Gs = [A16(120 + i, 512) for i in range(3)]
            t_Gs = [Tok() for _ in range(3)]
            acc = A32(124, T)
            t_acc = [Tok() for _ in range(3)]
            tmp = [A32(130 + 2 * i, 512) for i in range(3)]
            t_tmpm = [Tok() for _ in range(3)]
            kk = 0
            reload_x(range(12, 16))
            for fc in range(16):
                for k in range(4):
                    mod_drain(1)
                    wg, wgt = wload(w_in[l][:, OFF_G + k * D + fc * 128: OFF_G + k * D + (fc + 1) * 128], 16)
                    wb, wbt = wload(w_br[l, k][:, fc * 128:(fc + 1) * 128], 4)
                    for tgi, (t0, n) in enumerate(TGS):
                        bg = nextbank()
                        for kc in range(16):
                            mm(banks[bg][:, 0:n], wg[:, kc, :], hT[:, kc, t0:t0 + n], kc == 0, kc == 15, [wgt, t_hT[kc]], ptok[bg], ptok[bg])
                        bp = nextbank()
                        for kc in range(4):
                            mm(banks[bp][:, 0:n], wb[:, kc, :], oT[:, 4 * k + kc, t0:t0 + n], kc == 0, kc == 3, [wbt, t_oT[4 * k + kc]], ptok[bp], ptok[bp])
                        b = kk % 3
                        kk += 1
                        S.op("act", lambda e, o=Gs[b][:, 0:n], i=banks[bg][:, 0:n]: e.activation(out=o, in_=i, func=AF.Sigmoid),
                             reads=[ptok[bg]], writes=[t_Gs[b]])
                        a_ = acc[:, t0:t0 + n]
                        if k == 0:
                            S.op("dve", lambda e, o=a_, i0=banks[bp][:, 0:n], i1=Gs[b][:, 0:n]: e.tensor_tensor(out=o, in0=i0, in1=i1, op=ALU.mult),
                                 reads=[ptok[bp], t_Gs[b]], writes=[t_acc[tgi]])
                        else:
                            S.op("dve", lambda e, o=tmp[b][:, 0:n], i0=banks[bp][:, 0:n], i1=Gs[b][:, 0:n]: e.tensor_tensor(out=o, in0=i0, in1=i1, op=ALU.mult),
                                 reads=[ptok[bp], t_Gs[b]], writes=[t_tmpm[b]])
                            if k < 3:
                                S.op("dve", lambda e, o=a_, i1=tmp[b][:, 0:n]: e.tensor_tensor(out=o, in0=o, in1=i1, op=ALU.add),
                                     reads=[t_tmpm[b]], writes=[t_acc[tgi]])
                            else:
                                S.op("dve", lambda e, o=mgT[:, fc, t0:t0 + n], i0=a_, i1=tmp[b][:, 0:n]: e.tensor_tensor(out=o, in0=i0, in1=i1, op=ALU.add),
                                     reads=[t_tmpm[b], t_acc[tgi]], writes=[t_mg[fc]])
            S.flush()

        def stage_wffn(l, lim=99):
            reload_x(range(0, 12))
            perm = [12, 13, 14, 15] + list(range(12))

            def ev_res(jg):
                def f(blk, tgi, t0, n, bk):
                    s = 0 if tgi < 2 else 1
                    S.op("dve", lambda e, o=xT[:, blk, t0:t0 + n], i=banks[bk][:, 0:n], sc=modT[:, l, s, jg + blk:jg + blk + 1]:
                         e.scalar_tensor_tensor(out=o, in0=i, scalar=sc, in1=o, op0=ALU.mult, op1=ALU.add),
                         reads=[ptok[bk], t_modl[l]], writes=[t_xT[blk]])
                return f

            ev32 = ev_res(32)
            p1(lambda blk: w_out[l][:, perm[blk] * 128:(perm[blk] + 1) * 128], 16, 16,
               lambda kc, t0, n: mgT[:, kc, t0:t0 + n], lambda kc: [t_mg[kc]],
               lambda blk, tgi, t0, n, bk: ev32(perm[blk], tgi, t0, n, bk))
            S.flush()
            if lim < 6:
                return
            stage_norm(l, 1)
            S.flush()
            if lim < 7:
                return
            aT = A16(40, 16 * T).rearrange("p (f t) -> p f t", f=16)
            t_aT = [Tok() for _ in range(16)]
            rl = [A32(0, 1) for _ in range(0)]
            rtmp = [A32(184 + 2 * i, 512) for i in range(3)]
            t_rt = [Tok() for _ in range(3)]
            cnt = [0]
            for q in range(4):
                def ev_ff1(blk, tgi, t0, n, bk):
                    b = cnt[0] % 3
                    cnt[0] += 1
                    S.op("act", lambda e, o=rtmp[b][:, 0:n], i=banks[bk][:, 0:n]: e.activation(out=o, in_=i, func=AF.Relu),
                         reads=[ptok[bk]], writes=[t_rt[b]])
                    eng = "dve"
                    S.op(eng, lambda e, o=aT[:, blk, t0:t0 + n], i=rtmp[b][:, 0:n]: e.tensor_tensor(out=o, in0=i, in1=i, op=ALU.mult),
                         reads=[t_rt[b]], writes=[t_aT[blk]])
                p1(lambda blk: w_ff1[l][:, q * D + blk * 128: q * D + (blk + 1) * 128], 16, 16,
                   lambda kc, t0, n: hT[:, kc, t0:t0 + n], lambda kc: [t_hT[kc]], ev_ff1, modn=1)
                p1(lambda blk: w_ff2[l][q * D:(q + 1) * D, blk * 128:(blk + 1) * 128], 16, 16,
                   lambda kc, t0, n: aT[:, kc, t0:t0 + n], lambda kc: [t_aT[kc]], ev_res(80), modn=1)
            S.flush()

        bctr = {}

        def nb(key, lo, hi):
            c = bctr.get(key, 0)
            bctr[key] = c + 1
            return lo + c % (hi - lo)

        def bcast_row(dst, row_ap, n, tok):
            tt = bctr.setdefault("rowtok_small", Tok())
            tmp = small[0:1, 1100:1100 + n] if n <= 64 else None
            assert tmp is not None
            S.dma("sp", tmp, row_ap, writes=[tt])
            bk = nextbank()
            mm(banks[bk][:, 0:n], ones_f[0:1, :], tmp, True, True, [tt, t_const], ptok[bk], ptok[bk])
            S.op("dve", lambda e, o=dst, i=banks[bk][:, 0:n]: e.tensor_copy(out=o, in_=i), reads=[ptok[bk]], writes=[tok])

        def bcast_big(dst, row_ap, n, tok, tmpreg):
            tt = bctr.setdefault("rowtok_big", Tok())
            S.dma("sp", tmpreg[0:1, 0:n], row_ap, writes=[tt])
            bk = nextbank()
            mm(banks[bk][:, 0:n], ones_f[0:1, :], tmpreg[0:1, 0:n], True, True, [tt, t_const], ptok[bk], ptok[bk])
            S.op("dve", lambda e, o=dst, i=banks[bk][:, 0:n]: e.tensor_copy(out=o, in_=i), reads=[ptok[bk]], writes=[tok])

        def stage_attn(l):
            wq = A16(40, 16 * 768).rearrange("p (k n) -> p k n", k=16)
            t_wq = [Tok() for _ in range(3)]
            for c3 in range(3):
                S.dma("pool", wq[:, :, c3 * 256:(c3 + 1) * 256],
                      w_in[l][:, c3 * 256:(c3 + 1) * 256].rearrange("(k p) n -> p k n", p=128), writes=[t_wq[c3]])
            COS = A32(120, 640).rearrange("p (t d) -> p t d", t=10)
            SIN = A32(122.5, 640).rearrange("p (t d) -> p t d", t=10)
            gqk = A32(125, 640).rearrange("p (h d) -> p h d", h=10)
            t_rope = Tok()
            t_g = Tok()
            S.dma("sp", COS, ropec.rearrange("(t p) d -> p t d", p=128), writes=[t_rope])
            S.dma("sp", SIN, ropes.rearrange("(t p) d -> p t d", p=128), writes=[t_rope])
            gq = small[:, 0:64]
            gk = small[:, 64:128]
            bcast_row(gq, q_g[l:l + 1, :], 64, t_g)
            bcast_row(gk, k_g[l:l + 1, :], 64, t_g)
            S.op("dve", lambda e: e.tensor_copy(out=gqk[:, 0:8, :], in_=bc_in(gq, 8)), reads=[t_g], writes=[t_g])
            S.op("dve", lambda e: e.tensor_copy(out=gqk[:, 8:10, :], in_=bc_in(gk, 2)), reads=[t_g], writes=[t_g])
            qT = A16(127.5, 8 * T)[0:68, :].rearrange("p (h t) -> p h t", h=8)
            kT = A16(147.5, 2 * 1536)[0:68, :].rearrange("p (h t) -> p h t", h=2)
            vaug = A16(153.5, 12 * 130).rearrange("p (b k d) -> p b k d", b=12, k=2)
            o_tok = A16(64, 10 * 512).rearrange("p (t d) -> p t d", t=10)
            t_qT, t_kT, t_va, t_ot = Tok(), Tok(), Tok(), [Tok() for _ in range(10)]
            t_qm, t_km = Tok(), Tok()
            kctx = A16(116, 256).rearrange("p (b d) -> p b d", b=2)
            t_kc = Tok()
            S.dma("pool", kctx, ctxk[l].rearrange("(b p) d -> p b d", p=128), writes=[t_kc])
            for blk in range(2):
                S.dma("pool", vaug[:, blk, :, 0:64], ctxv[l][blk * 128:(blk + 1) * 128, :].rearrange("p (k d) -> p k d", k=2), writes=[t_va])
            S.op("dve", lambda e: e.memset(vaug[:, :, :, 64:65], 1.0), writes=[t_va])
            for kv in range(2):
                S.dma("pool", kT[64:68, kv, :], kmask, writes=[t_km])
            for h in range(8):
                S.dma("pool", qT[64:68, h, :], qmask, writes=[t_qm])
            bk = nextbank()
            for blk in range(2):
                for kv in range(2):
                    i4 = blk * 2 + kv
                    tr(bankbf(bk)[0:64, i4 * 128:(i4 + 1) * 128], kctx[:, blk, kv * 64:(kv + 1) * 64], identb,
                       [t_kc], ptok[bk], ptok[bk], i4 == 3, i4 == 0)
            for blk in range(2):
                S.op("dve", lambda e, o=kT[0:64, :, blk * 128:(blk + 1) * 128], i=bankbf(bk)[0:64, blk * 256:(blk + 1) * 256].rearrange("p (k c) -> p k c", k=2):
                     e.tensor_copy(out=o, in_=i), reads=[ptok[bk]], writes=[t_kT])
            def a1_mm(t):
                par = t % 2
                bq, bkv = 4 * par, 4 * par + 1
                cols = slice(t * 128, (t + 1) * 128)
                for kc in range(16):
                    mm(banks[bq][:, :], hT[:, kc, cols], wq[:, kc, 0:512], kc == 0, kc == 15, [t_hT[kc], t_wq[0], t_wq[1]], ptok[bq], ptok[bq])
                for kc in range(16):
                    mm(banks[bkv][:, 0:256], hT[:, kc, cols], wq[:, kc, 512:768], kc == 0, kc == 15, [t_hT[kc], t_wq[2]], ptok[bkv], ptok[bkv])

            def a1_rest(t, phase):
                par = t % 2
                base = 90 + 13 * par
                sq = A32(base, 640)
                qn = A32(base + 2.5, 640)
                Aa = A32(base + 5, 640)
                Bm = A32(base + 7.5, 640)
                qr = A16(base + 10, 640)
                vt = A32(base + 11.25, 128)
                ss = A32(base + 11.75, 16)[:, 0:10]
                rs = A32(base + 11.875, 16)[:, 0:10]
                tk = bctr.setdefault(("attok", par), [Tok() for _ in range(8)])
                t_sq, t_qn, t_Aa, t_Bm, t_qr, t_vt, t_ss, t_rs = tk
                bq, bkv, btq, btk = 4 * par, 4 * par + 1, 4 * par + 2, 4 * par + 3
                cols = slice(t * 128, (t + 1) * 128)
                if phase == 1:
                    S.op("act", lambda e, o=sq[:, 0:512], i=banks[bq][:, :]: e.activation(out=o, in_=i, func=AF.Square), reads=[ptok[bq]], writes=[t_sq])
                    S.op("act", lambda e, o=sq[:, 512:640], i=banks[bkv][:, 0:128]: e.activation(out=o, in_=i, func=AF.Square), reads=[ptok[bkv]], writes=[t_sq])
                    S.op("dve", lambda e, o=ss, i=sq.rearrange("p (h d) -> p h d", h=10): e.reduce_sum(out=o, in_=i, axis=AX.X), reads=[t_sq], writes=[t_ss])
                    S.op("act", lambda e, o=rs, i=ss: e.activation(out=o, in_=i, func=AF.Sqrt, scale=1.0 / 64, bias=epsc), reads=[t_ss, t_const], writes=[t_rs])
                    S.op("dve", lambda e, o=rs: e.reciprocal(out=o, in_=o), reads=[t_rs], writes=[t_rs])
                    S.op("dve", lambda e, o=qn[:, 0:512].rearrange("p (h d) -> p h d", h=8), i=banks[bq][:, :].rearrange("p (h d) -> p h d", h=8), r=bc_mid(rs[:, 0:8], 64):
                         e.tensor_tensor(out=o, in0=i, in1=r, op=ALU.mult), reads=[ptok[bq], t_rs], writes=[t_qn])
                    S.op("dve", lambda e, o=qn[:, 512:640].rearrange("p (h d) -> p h d", h=2), i=banks[bkv][:, 0:128].rearrange("p (h d) -> p h d", h=2), r=bc_mid(rs[:, 8:10], 64):
                         e.tensor_tensor(out=o, in0=i, in1=r, op=ALU.mult), reads=[ptok[bkv], t_rs], writes=[t_qn])
                    S.op("pool", lambda e, o=qn, g_=gqk.rearrange("p h d -> p (h d)"): e.tensor_tensor(out=o, in0=o, in1=g_, op=ALU.mult), reads=[t_g], writes=[t_qn])
                    S.dma("sp", nk_o[l, t * 128:(t + 1) * 128, :], qn[:, 512:640], reads=[t_qn])
                    S.op("act", lambda e, o=vt, i=banks[bkv][:, 128:256]: e.copy(out=o, in_=i), reads=[ptok[bkv]], writes=[t_vt])
                    S.dma("sp", nv_o[l, t * 128:(t + 1) * 128, :], vt, reads=[t_vt])
                    S.op("dve", lambda e, o=vaug[:, 2 + t, :, 0:64], i=banks[bkv][:, 128:256].rearrange("p (k d) -> p k d", k=2): e.tensor_copy(out=o, in_=i),
                         reads=[ptok[bkv]], writes=[t_va])
                if phase == 1:
                    return
                qn3 = qn.rearrange("p (h d) -> p h d", h=10)
                S.op("pool", lambda e, o=Aa.rearrange("p (h d) -> p h d", h=10), i=qn3, c_=bc_in(COS[:, t, :], 10): e.tensor_tensor(out=o, in0=i, in1=c_, op=ALU.mult),
                     reads=[t_qn, t_rope], writes=[t_Aa])
                qn5 = qn.rearrange("p (h r a d) -> p h r a d", h=10, r=2, a=2)
                Bm5 = Bm.rearrange("p (h r a d) -> p h r a d", h=10, r=2, a=2)
                sn4 = SIN[:, t, :].rearrange("p (r a d) -> p r a d", r=2, a=2)
                for a_ in range(2):
                    S.op("dve", lambda e, o=Bm5[:, :, :, a_, :], i=qn5[:, :, :, 1 - a_, :], s_=sn4[:, :, a_, :].unsqueeze(1).to_broadcast([128, 10, 2, 16]):
                         e.tensor_tensor(out=o, in0=i, in1=s_, op=ALU.mult), reads=[t_qn, t_rope], writes=[t_Bm])
                S.op("dve", lambda e, o=qr, i0=Aa, i1=Bm: e.tensor_tensor(out=o, in0=i0, in1=i1, op=ALU.add), reads=[t_Aa, t_Bm], writes=[t_qr])
                for h in range(8):
                    tr(bankbf(btq)[0:64, h * 128:(h + 1) * 128], qr[:, h * 64:(h + 1) * 64], identb, [t_qr], ptok[btq], ptok[btq], h == 7, h == 0)
                for kv in range(2):
                    tr(bankbf(btk)[0:64, kv * 128:(kv + 1) * 128], qr[:, 512 + kv * 64:512 + (kv + 1) * 64], identb, [t_qr], ptok[btk], ptok[btk], kv == 1, kv == 0)
                S.op("act", lambda e, o=qT[0:64, :, cols], i=bankbf(btq)[0:64, :].rearrange("p (h c) -> p h c", h=8): e.copy(out=o, in_=i),
                     reads=[ptok[btq]], writes=[t_qT])
                S.op("dve", lambda e, o=kT[0:64, :, 256 + t * 128:256 + (t + 1) * 128], i=bankbf(btk)[0:64, 0:256].rearrange("p (h c) -> p h c", h=2): e.tensor_copy(out=o, in_=i),
                     reads=[ptok[btk]], writes=[t_kT])

            a1_mm(0)
            a1_mm(1)
            a1_rest(0, 1)
            for t in range(NT):
                if t + 2 < NT:
                    a1_mm(t + 2)
                if t + 1 < NT:
                    a1_rest(t + 1, 1)
                a1_rest(t, 2)
            tap("qT", qT, [t_qT, t_qm])
            tap("kT", kT, [t_kT, t_km])
            S.flush()
            PT = [A16(40 + 10 * i, 10 * 512).rearrange("p (b q) -> p b q", b=10) for i in range(2)]
            t_PT = [[Tok() for _ in range(10)] for _ in range(2)]
            rd = [small[:, 216 + 4 * i:220 + 4 * i] for i in range(2)]
            t_rd = [Tok(), Tok()]
            it = 0
            for h in range(8):
                kv = h // 4
                for (q0, n, kbs, K) in ((0, 512, list(range(10)), 68), (512, 512, list(range(10)), 68), (1024, 256, [10, 11], 64)):
                    buf = it % 2
                    it += 1
                    for kbi, kb in enumerate(kbs):
                        bs = nb("sc", 0, 4)
                        mm(banks[bs][:, 0:n], kT[0:K, kv, kb * 128:(kb + 1) * 128], qT[0:K, h, q0:q0 + n], True, True,
                           [t_kT, t_qT, t_km, t_qm], ptok[bs], ptok[bs])
                        S.op("act", lambda e, o=PT[buf][:, kbi, 0:n], i=banks[bs][:, 0:n]: e.activation(out=o, in_=i, func=AF.Exp, scale=0.125),
                             reads=[ptok[bs]], writes=[t_PT[buf][kbi]])
                    bo = nb("pv", 4, 8)
                    nq = n // 128
                    for qs in range(nq):
                        for kbi, kb in enumerate(kbs):
                            first = (qs == 0 and kbi == 0)
                            last = (qs == nq - 1 and kbi == len(kbs) - 1)
                            S.op("pe", lambda e, o=banks[bo][:, qs * 128:qs * 128 + 65], a=PT[buf][:, kbi, qs * 128:(qs + 1) * 128], b=vaug[:, kb, kv, :], st=(kbi == 0), sp=(kbi == len(kbs) - 1):
                                 e.matmul(o, lhsT=a, rhs=b, start=st, stop=sp),
                                 reads=[t_PT[buf][kbi], t_va], writes=[ptok[bo]] if last else [], guard=[ptok[bo]] if first else [], sig=last)
                    pv3 = banks[bo][:, :].rearrange("p (q d) -> p q d", q=4)
                    S.op("dve", lambda e, o=rd[buf][:, 0:nq].unsqueeze(2), i=pv3[:, 0:nq, 64:65]: e.reciprocal(out=o, in_=i), reads=[ptok[bo]], writes=[t_rd[buf]])
                    t0i = q0 // 128
                    S.op("dve", lambda e, o=o_tok[:, t0i:t0i + nq, h * 64:(h + 1) * 64], i=pv3[:, 0:nq, 0:64], r=bc_mid(rd[buf][:, 0:nq], 64):
                         e.tensor_tensor(out=o, in0=i, in1=r, op=ALU.mult), reads=[ptok[bo], t_rd[buf]], writes=t_ot[t0i:t0i + nq])
            for t in range(NT):
                bk = nextbank()
                for j in range(4):
                    tr(bankbf(bk)[:, j * 128:(j + 1) * 128], o_tok[:, t, j * 128:(j + 1) * 128], identb, [t_ot[t]], ptok[bk], ptok[bk], j == 3, j == 0)
                copy_any(alt(), oT[:, 0:4, t * 128:(t + 1) * 128], bankbf(bk)[:, 0:512].rearrange("p (j c) -> p j c", j=4), [ptok[bk]], t_oT[0:4])
            S.flush()

        def stage_fourier(l):
            fT = A16(40, 4 * T).rearrange("p (g t) -> p g t", g=4)
            t_fT = [Tok() for _ in range(4)]
            G = A16(50, 10 * 1024).rearrange("p (t g c) -> p t g c", t=10, g=4)
            t_G = [Tok() for _ in range(10)]
            CS = A16(70, 256)
            TA = A16(120, 8 * 2 * 1024).rearrange("p (b c n) -> p b c n", b=8, c=2)
            TB = A16(152, 2 * 2 * 256).rearrange("p (b c n) -> p b c n", b=2, c=2)
            t_cs, t_ta, t_tb = Tok(), [Tok(), Tok()], Tok()
            def ev_f(blk, tgi, t0, n, bk):
                copy_any(alt(), fT[:, blk, t0:t0 + n], banks[bk][:, 0:n], [ptok[bk]], [t_fT[blk]])

            p1(lambda g: w_in[l][:, OFF_F + g * 128:OFF_F + (g + 1) * 128], 4, 16,
               lambda kc, t0, n: hT[:, kc, t0:t0 + n], lambda kc: [t_hT[kc]], ev_f)
            S.dma("pool", CS, dftC, writes=[t_cs])
            t_ta = [[Tok() for _ in range(2)] for _ in range(8)]
            for b_ in range(8):
                for c in range(2):
                    S.dma("pool", TA[:, b_, c, :], dftA[c][b_ * 128:(b_ + 1) * 128, :], writes=[t_ta[b_][c]])
            for c in range(2):
                S.dma("pool", TB[:, :, c, :], dftB[c].rearrange("(b p) n -> p b n", p=128), writes=[t_tb])

            for t in range(NT):
                for half in range(2):
                    bk = nextbank()
                    for gi in range(2):
                        g = half * 2 + gi
                        mm(banks[bk][:, gi * 256:(gi + 1) * 256], fT[:, g, t * 128:(t + 1) * 128], CS, True, True, [t_fT[g], t_cs],
                           ptok[bk] if gi == 1 else None, ptok[bk] if gi == 0 else None)
                    copy_any(alt(), G[:, t, half * 2:half * 2 + 2, :], banks[bk][:, :].rearrange("p (g c) -> p g c", g=2), [ptok[bk]], [t_G[t]])
            for g in range(4):
                for half in range(2):
                    bk = nextbank()
                    i = 0
                    for lt in range(8):
                        for c in range(2):
                            mm(banks[bk][:, :], G[:, lt, g, c * 128:(c + 1) * 128], TA[:, lt, c, half * 512:(half + 1) * 512], i == 0, i == 15,
                               [t_G[lt], t_ta[lt][c]], ptok[bk], ptok[bk])
                            i += 1
                    copy_any(alt(), oT[:, 4 + g, half * 512:(half + 1) * 512], banks[bk][:, :], [ptok[bk]], [t_oT[4 + g]])
                bk = nextbank()
                i = 0
                for lt in range(2):
                    for c in range(2):
                        mm(banks[bk][:, 0:256], G[:, 8 + lt, g, c * 128:(c + 1) * 128], TB[:, lt, c, :], i == 0, i == 3, [t_G[8 + lt], t_tb], ptok[bk], ptok[bk])
                        i += 1
                copy_any(alt(), oT[:, 4 + g, 1024:1280], banks[bk][:, 0:256], [ptok[bk]], [t_oT[4 + g]])
            S.flush()

        def gelu_block(ps, n, out_ap, tA, tB, tokA, tokB, ptk, wtoks):
            S.op("act", lambda e, o=tA[:, 0:n], i=ps: e.activation(out=o, in_=i, func=AF.Square), reads=[ptk], writes=[tokA])
            S.op("dve", lambda e, o=tA[:, 0:n]: e.tensor_scalar(out=o, in0=o, scalar1=0.044715, scalar2=1.0, op0=ALU.mult, op1=ALU.add), writes=[tokA])
            S.op("dve", lambda e, o=tA[:, 0:n], i=ps: e.tensor_tensor(out=o, in0=o, in1=i, op=ALU.mult), reads=[ptk], writes=[tokA])
            S.op("act", lambda e, o=tB[:, 0:n], i=tA[:, 0:n]: e.activation(out=o, in_=i, func=AF.Sigmoid, scale=1.5957691216057308), reads=[tokA], writes=[tokB])
            S.op("dve", lambda e, o=out_ap, i0=tB[:, 0:n], i1=ps: e.tensor_tensor(out=o, in0=i0, in1=i1, op=ALU.mult), reads=[tokB, ptk], writes=wtoks)

        def stage_gmlp(l):
            uT = A16(40, 4 * T).rearrange("p (g t) -> p g t", g=4)
            t_uT = [Tok() for _ in range(4)]
            v_tok = A16(50, 10 * 512).rearrange("p (t d) -> p t d", t=10)
            t_v = [Tok() for _ in range(10)]
            wsT = A16(60, 512).rearrange("p (g i) -> p g i", g=4)
            bsbc = A32(61, 512)
            ggm = A32(63, 512)
            wsl = A32(65, 512).rearrange("p (g j) -> p g j", g=4)
            gt = [(A32(67 + 4 * i, 512), A32(69 + 4 * i, 512), Tok(), Tok()) for i in range(3)] + \
                 [(A32(150 + 4 * i, 512), A32(152 + 4 * i, 512), Tok(), Tok()) for i in range(2)]
            wv = A16(120, 16 * 512).rearrange("p (k n) -> p k n", k=16)
            t_wv = [Tok(), Tok()]
            vg = [A32(136 + 2 * i, 512) for i in range(2)]
            t_vg = [Tok(), Tok()]
            sqj = [A32(140, 512), A32(148, 512)]
            t_sqj = [Tok(), Tok()]
            tm = [A32(142 + 2 * i, 512) for i in range(2)]
            t_tm = [Tok(), Tok()]
            rowt = A32(146, 512)
            ssq = [small[:, 224 + 2 * i:225 + 2 * i] for i in range(2)]
            t_ssq = [Tok(), Tok()]
            t_ws, t_bs, t_gg = Tok(), Tok(), Tok()
            for c2 in range(2):
                S.dma("pool", wv[:, :, c2 * 256:(c2 + 1) * 256],
                      w_in[l][:, OFF_UV + 512 + c2 * 256:OFF_UV + 512 + (c2 + 1) * 256].rearrange("(k p) n -> p k n", p=128), writes=[t_wv[c2]])
            S.dma("sp", wsl, w_sp[l].rearrange("g i j -> i g j"), writes=[t_ws])
            bcast_big(bsbc, b_sp[l:l + 1, :], 512, t_bs, rowt)
            bcast_big(ggm, gmlp_g[l:l + 1, :], 512, t_gg, rowt)
            bk = nextbank()
            for g in range(4):
                tr(banks[bk][:, g * 128:(g + 1) * 128], wsl[:, g, :], ident_f, [t_ws], ptok[bk], ptok[bk], g == 3, g == 0)
            t_wsT = Tok()
            S.op("dve", lambda e, o=wsT.rearrange("p g i -> p (g i)"), i=banks[bk][:, :]: e.tensor_copy(out=o, in_=i), reads=[ptok[bk]], writes=[t_wsT])
            gi_ = [0]

            def ev_u(blk, tgi, t0, n, bk):
                a, b, ta, tb_ = gt[gi_[0] % 5]
                gi_[0] += 1
                gelu_block(banks[bk][:, 0:n], n, uT[:, blk, t0:t0 + n], a, b, ta, tb_, ptok[bk], [t_uT[blk]])
                vstep()

            def gv1(t):
                b2 = t % 2
                bk = nextbank()
                for kc in range(16):
                    mm(banks[bk][:, :], hT[:, kc, t * 128:(t + 1) * 128], wv[:, kc, :], kc == 0, kc == 15, [t_hT[kc], t_wv[0], t_wv[1]], ptok[bk], ptok[bk])
                a, b, ta, tb_ = gt[gi_[0] % 5]
                gi_[0] += 1
                gelu_block(banks[bk][:, :], 512, vg[b2], a, b, ta, tb_, ptok[bk], [t_vg[b2]])

            def gv2(t):
                b2 = t % 2
                S.op("act", lambda e, o=sqj[b2], i=vg[b2]: e.activation(out=o, in_=i, func=AF.Square), reads=[t_vg[b2]], writes=[t_sqj[b2]])
                S.op("dve", lambda e, o=ssq[b2], i=sqj[b2]: e.reduce_sum(out=o, in_=i, axis=AX.X), reads=[t_sqj[b2]], writes=[t_ssq[b2]])
                S.op("act", lambda e, o=ssq[b2]: e.activation(out=o, in_=o, func=AF.Sqrt, scale=1.0 / 512, bias=epsc), reads=[t_const], writes=[t_ssq[b2]])
                S.op("dve", lambda e, o=ssq[b2]: e.reciprocal(out=o, in_=o), writes=[t_ssq[b2]])
                S.op("dve", lambda e, o=v_tok[:, t, :], i=vg[b2], sc=ssq[b2], g_=ggm: e.scalar_tensor_tensor(out=o, in0=i, scalar=sc, in1=g_, op0=ALU.mult, op1=ALU.mult),
                     reads=[t_vg[b2], t_ssq[b2], t_gg], writes=[t_v[t]])


            vstate = [0]

            def vstep():
                t = vstate[0]
                if t > NT:
                    return
                if t == 0:
                    gv1(0)
                else:
                    if t < NT:
                        gv1(t)
                    gv2(t - 1)
                vstate[0] += 1

            p1(lambda g: w_in[l][:, OFF_UV + g * 128:OFF_UV + (g + 1) * 128], 4, 16,
               lambda kc, t0, n: hT[:, kc, t0:t0 + n], lambda kc: [t_hT[kc]], ev_u)
            while vstate[0] <= NT:
                vstep()
            for c in range(NT):
                b2 = c % 2
                bk = nextbank()
                for g in range(4):
                    mm(banks[bk][:, g * 128:(g + 1) * 128], v_tok[:, c, g * 128:(g + 1) * 128], wsT[:, g, :], True, True, [t_v[c], t_wsT],
                       ptok[bk] if g == 3 else None, ptok[bk] if g == 0 else None)
                S.op("dve", lambda e, o=tm[b2], i=banks[bk][:, :], b_=bsbc: e.tensor_tensor(out=o, in0=i, in1=b_, op=ALU.add), reads=[ptok[bk], t_bs], writes=[t_tm[b2]])
                S.op("pool" if c % 2 else "dve", lambda e, o=oT[:, 12:16, c * 128:(c + 1) * 128], i=tm[b2].rearrange("p (g i) -> p g i", g=4), u_=uT[:, :, c * 128:(c + 1) * 128]:
                     e.tensor_tensor(out=o, in0=i, in1=u_, op=ALU.mult), reads=[t_tm[b2]] + t_uT, writes=t_oT[12:16])
            S.flush()

        def stage_ssm(l):
            samp = flg[:, 0:1]
            nsamp = flg[:, 1:2]
            convw = small[:, 128:152].rearrange("p (k c) -> p k c", k=3)
            convb = small[:, 152:160]
            nw0 = small[:, 160:168]
            nw2 = small[:, 168:176]
            dtb = small[:, 176:192]
            Abc = small[:, 192:208]
            dsk = small[:, 208:216]
            wgt = small[:, 232:248]
            dtv_all = small[:, 256:416].rearrange("p (t d) -> p t d", t=10)
            dtA_all = small[:, 416:576].rearrange("p (t d) -> p t d", t=10)
            E_all = small[:, 576:1056].rearrange("p (t d) -> p t d", t=10)
            t_cw, t_dtp, t_dt, t_E = Tok(), Tok(), [Tok() for _ in range(10)], [Tok() for _ in range(10)]
            xbcT = A16(40, 8 * T).rearrange("p (c t) -> p c t", c=8)
            t_xbc = [Tok() for _ in range(8)]
            z_tok = A16(60, 10 * 512).rearrange("p (t d) -> p t d", t=10)
            t_z = [Tok() for _ in range(10)]
            gssm = A32(70, 512)
            t_gs = Tok()
            raw = [A32(120 + 5 * i, T) for i in range(2)]
            acc = [A32(130 + 5 * i, T) for i in range(2)]
            t_raw, t_acc = [Tok(), Tok()], [Tok(), Tok()]
            wz = A16(140, 16 * 528).rearrange("p (k n) -> p k n", k=16)
            t_wz = [Tok(), Tok(), Tok()]
            rowt = A32(157, 512)
            for c3, (a, b) in enumerate(((0, 256), (256, 512), (512, 528))):
                S.dma("pool", wz[:, :, a:b], w_in[l][:, OFF_Z + a:OFF_Z + b].rearrange("(k p) n -> p k n", p=128), writes=[t_wz[c3]])
            load_T(small[:, 128:152], conv_w[l].rearrange("k (c p) -> (k c) p", p=128), 24, A32(100, 128), t_cw)
            load_T(convb, conv_b[l].rearrange("(c p) -> c p", p=128), 8, A32(101, 128), t_cw)
            S.op("dve", lambda e: e.tensor_scalar(out=nw0, in0=convw[:, 0, :], scalar1=nsamp, scalar2=-1.0, op0=ALU.mult, op1=ALU.mult), reads=[t_cw, t_const], writes=[t_cw])
            S.op("dve", lambda e: e.tensor_scalar(out=nw2, in0=convw[:, 2, :], scalar1=nsamp, scalar2=-1.0, op0=ALU.mult, op1=ALU.mult), reads=[t_cw, t_const], writes=[t_cw])
            bcast_row(dtb, dt_bias[l:l + 1, :], 16, t_dtp)
            bcast_row(Abc, a_log[l:l + 1, :], 16, t_dtp)
            S.op("act", lambda e: e.activation(out=Abc, in_=Abc, func=AF.Exp), writes=[t_dtp])
            S.op("dve", lambda e: e.tensor_scalar_mul(out=Abc, in0=Abc, scalar1=-1.0), writes=[t_dtp])
            bcast_row(dsk, d_skip[l:l + 1, :], 8, t_dtp)
            bcast_big(gssm, ssm_g[l:l + 1, :], 512, t_gs, rowt)

            def ev_x(blk, tgi, t0, n, bk):
                b = blk % 2
                S.op("act", lambda e, o=raw[b][:, t0:t0 + n], i=banks[bk][:, 0:n]: e.copy(out=o, in_=i), reads=[ptok[bk]], writes=[t_raw[b]])
                if tgi < 2:
                    return
                r, a = raw[b], acc[b]
                S.op("dve", lambda e: e.tensor_scalar(out=a, in0=r, scalar1=convw[:, 1, blk:blk + 1], scalar2=convb[:, blk:blk + 1], op0=ALU.mult, op1=ALU.add),
                     reads=[t_raw[b], t_cw], writes=[t_acc[b]])
                for (s0, s1) in ((0, 1024), (1024, 1280)):
                    S.op("dve", lambda e, o=a[:, s0 + 1:s1], i=r[:, s0:s1 - 1]: e.scalar_tensor_tensor(out=o, in0=i, scalar=convw[:, 0, blk:blk + 1], in1=o, op0=ALU.mult, op1=ALU.add),
                         reads=[t_raw[b], t_cw], writes=[t_acc[b]])
                    S.op("dve", lambda e, o=a[:, s0:s1 - 1], i=r[:, s0 + 1:s1]: e.scalar_tensor_tensor(out=o, in0=i, scalar=convw[:, 2, blk:blk + 1], in1=o, op0=ALU.mult, op1=ALU.add),
                         reads=[t_raw[b], t_cw], writes=[t_acc[b]])
                S.op("dve", lambda e, o=a[:, 256:1024:256], i=r[:, 255:1023:256]: e.scalar_tensor_tensor(out=o, in0=i, scalar=nw0[:, blk:blk + 1], in1=o, op0=ALU.mult, op1=ALU.add),
                     reads=[t_raw[b], t_cw], writes=[t_acc[b]])
                S.op("dve", lambda e, o=a[:, 255:1023:256], i=r[:, 256:1024:256]: e.scalar_tensor_tensor(out=o, in0=i, scalar=nw2[:, blk:blk + 1], in1=o, op0=ALU.mult, op1=ALU.add),
                     reads=[t_raw[b], t_cw], writes=[t_acc[b]])
                S.op("act", lambda e, o=xbcT[:, blk, :], i=a: e.activation(out=o, in_=i, func=AF.Silu), reads=[t_acc[b]], writes=[t_xbc[blk]])

            p1(lambda cc: w_in[l][:, OFF_XBC + cc * 128:OFF_XBC + (cc + 1) * 128], 8, 16,
               lambda kc, t0, n: hT[:, kc, t0:t0 + n], lambda kc: [t_hT[kc]], ev_x)
            for t in range(NT):
                bz = nextbank()
                for kc in range(16):
                    mm(banks[bz][:, :], hT[:, kc, t * 128:(t + 1) * 128], wz[:, kc, 0:512], kc == 0, kc == 15, [t_hT[kc], t_wz[0], t_wz[1]], ptok[bz], ptok[bz])
                bd = nextbank()
                for kc in range(16):
                    mm(banks[bd][:, 0:16], hT[:, kc, t * 128:(t + 1) * 128], wz[:, kc, 512:528], kc == 0, kc == 15, [t_hT[kc], t_wz[2]], ptok[bd], ptok[bd])
                S.op("act", lambda e, o=z_tok[:, t, :], i=banks[bz][:, :]: e.activation(out=o, in_=i, func=AF.Silu), reads=[ptok[bz]], writes=[t_z[t]])
                S.op("dve", lambda e, o=dtv_all[:, t, :], i=banks[bd][:, 0:16]: e.tensor_tensor(out=o, in0=i, in1=dtb, op=ALU.add), reads=[ptok[bd], t_dtp], writes=[t_dt[t]])
                S.op("act", lambda e, o=dtv_all[:, t, :]: e.activation(out=o, in_=o, func=AF.Exp), writes=[t_dt[t]])
                S.op("act", lambda e, o=dtv_all[:, t, :]: e.activation(out=o, in_=o, func=AF.Ln, bias=1.0), writes=[t_dt[t]])
                S.op("dve", lambda e, o=dtA_all[:, t, :], i=dtv_all[:, t, :]: e.tensor_tensor(out=o, in0=i, in1=Abc, op=ALU.mult), reads=[t_dtp], writes=[t_dt[t]])
            tap("xbcT", xbcT, t_xbc)
            tap("dtv", small[:, 256:416], t_dt)
            S.flush()
            cs_b = A32(120, 10 * 512).rearrange("p (t d) -> p t d", t=10)
            t_csb = [Tok() for _ in range(10)]
            y_part = A16(140, 10 * 512).rearrange("p (t d) -> p t d", t=10)
            t_yp = [Tok() for _ in range(10)]
            ltm = A32(100, 16 * 128).rearrange("p (h j) -> p h j", h=16)
            MTb = [A16(108 + 4 * i, 16 * 128).rearrange("p (h j) -> p h j", h=16) for i in range(2)]
            cbm = A16(116, 512).rearrange("p (d g i) -> p d g i", d=2, g=2)
            xsb = [A16(117 + i, 512) for i in range(2)]
            Btb = [A16(119 + 0.5 * i, 256) for i in range(2)]
            dtxb = [A16(150, 1024).rearrange("p (d n) -> p d n", d=2), A16(156, 1024).rearrange("p (d n) -> p d n", d=2)]
            Wdb = [A16(152, 1024).rearrange("p (d n) -> p d n", d=2), A16(158, 1024).rearrange("p (d n) -> p d n", d=2)]
            ytmp = A32(154, 512)
            Sst = A32(72, 512)
            Sbf = A16(74, 512)
            Sn = A32(75, 512)
            stf = A32(77, 512).rearrange("p (j n) -> p j n", j=4)
            hl = A32(158, 512).rearrange("p (j n) -> p j n", j=4)
            t_lt, t_cbm, t_y1, t_wg = Tok(), Tok(), Tok(), [Tok(), Tok()]
            t_MT, t_xs, t_Bt, t_dtx, t_W = ([Tok(), Tok()] for _ in range(5))
            t_S, t_Sbf, t_Sn, t_stf = [Tok() for _ in range(4)]
            t_hl = t_W[1]
            wgtb = [small[:, 232:248], small[:, 1180:1196]]
            v8 = lambda ap: ap.rearrange("p (h d) -> p h d", h=8)

            def load_state(dirn, dst, dst_tok):
                S.dma("sp", hl, h0[l, dirn].rearrange("(j p) n -> p j n", p=128), writes=[t_hl])
                bk = nextbank()
                for j in range(4):
                    tr(banks[bk][:, j * 128:(j + 1) * 128], hl[:, j, :], ident_f, [t_hl], ptok[bk], ptok[bk], j == 3, j == 0)
                S.op("dve", lambda e, o=dst, i=banks[bk][:, :]: e.tensor_copy(out=o, in_=i), reads=[ptok[bk]], writes=[dst_tok])

            def store_state(src, src_tok, seq, dirn, bk):
                for j in range(4):
                    tr(banks[bk][:, j * 128:(j + 1) * 128], src[:, j * 128:(j + 1) * 128], ident_f, [src_tok], ptok[bk], ptok[bk], j == 3, j == 0)
                S.op("act", lambda e, o=stf.rearrange("p j n -> p (j n)"), i=banks[bk][:, :]: e.copy(out=o, in_=i), reads=[ptok[bk]], writes=[t_stf])
                S.dma("sp", ns_o[l, seq, dirn].rearrange("(j p) n -> p j n", p=128), stf, reads=[t_stf])

            load_state(0, Sst, t_S)
            S.op("act", lambda e: e.copy(out=Sbf, in_=Sst), reads=[t_S], writes=[t_Sbf])

            def m2_front(c):
                pb_ = c % 2
                MT, xs_tok, B_tok, dtx, Wd, wgt = MTb[pb_], xsb[pb_], Btb[pb_], dtxb[pb_], Wdb[pb_], wgtb[pb_]
                cols = slice(c * 128, (c + 1) * 128)
                b0 = 0
                for j in range(6):
                    tr(bankbf(b0)[:, j * 128:(j + 1) * 128], xbcT[:, j, cols], identb, [t_xbc[j]], ptok[b0], ptok[b0], j == 5, j == 0)
                S.op("act", lambda e, i=bankbf(b0)[:, 0:512]: e.copy(out=xs_tok, in_=i), reads=[ptok[b0]], writes=[t_xs[pb_]])
                S.op("act", lambda e, i=bankbf(b0)[:, 512:768]: e.copy(out=B_tok, in_=i), reads=[ptok[b0]], writes=[t_Bt[pb_]])
                b1 = 1
                dA = dtA_all[:, c, :]
                specs = ((0, 0, 0, 8), (8, 2, 8, 16), (16, 1, 0, 8), (24, 3, 8, 16))
                for si, (o0, ti, a0, a1) in enumerate(specs):
                    mm(banks[b1][:, o0:o0 + 8], tri[:, ti, :], dA[:, a0:a1], True, True, [t_dt[c], t_const], None, ptok[b1] if si == 0 else None)
                mm(banks[b1][:, 32:48], ones_f, dA, True, True, [t_dt[c], t_const], ptok[b1], None)
                S.op("act", lambda e, o=E_all[:, c, :], i=banks[b1][:, 0:48]: e.activation(out=o, in_=i, func=AF.Exp), reads=[ptok[b1]], writes=[t_E[c]])
                for g in range(2):
                    mm(banks[b1][:, 128 + g * 128:256 + g * 128], xbcT[:, 4 + g, cols], xbcT[:, 6 + g, cols], True, True, [t_xbc[4 + g], t_xbc[6 + g]],
                       ptok[b1] if g == 1 else None, ptok[b1] if g == 0 else None)
                for d in range(2):
                    S.op("dve", lambda e, o=cbm[:, d, :, :], i=banks[b1][:, 128:384].rearrange("p (g i) -> p g i", g=2), m=bc_in(tribf[:, d, :], 2):
                         e.tensor_tensor(out=o, in0=i, in1=m, op=ALU.mult), reads=[ptok[b1], t_const], writes=[t_cbm])
                for d in range(2):
                    S.op("pool", lambda e, o=ltm[:, d * 8:d * 8 + 8, :], m=bc_in(tri[:, 1 if d == 0 else 3, :], 8), d_=bc_mid(dA[:, d * 8:d * 8 + 8], 128): e.tensor_tensor(out=o, in0=m, in1=d_, op=ALU.mult),
                         reads=[t_dt[c], t_const], writes=[t_lt])
                    for hh in range(2):
                        bk = 2 + hh
                        for h4 in range(4):
                            hd = d * 8 + hh * 4 + h4
                            mm(banks[bk][:, h4 * 128:(h4 + 1) * 128], ltm[:, hd, :], tri[:, 0 if d == 0 else 2, :], True, True, [t_lt, t_const],
                               ptok[bk] if h4 == 3 else None, ptok[bk] if h4 == 0 else None)
                        h0_ = d * 8 + hh * 4
                        S.op("act", lambda e, o=MT[:, h0_:h0_ + 4, :].rearrange("p h j -> p (h j)"), i=banks[bk][:, :]: e.activation(out=o, in_=i, func=AF.Exp),
                             reads=[ptok[bk]], writes=[t_MT[pb_]])
                        g = hh
                        S.op("dve", lambda e, o=MT[:, h0_:h0_ + 4, :], m=bc_in(cbm[:, d, g, :], 4): e.tensor_tensor(out=o, in0=o, in1=m, op=ALU.mult),
                             reads=[t_cbm], writes=[t_MT[pb_]])
                S.op("dve", lambda e, o=wgt, i=dtv_all[:, c, :], ee=E_all[:, c, 16:32]: e.tensor_tensor(out=o, in0=i, in1=ee, op=ALU.mult), reads=[t_dt[c], t_E[c]], writes=[t_wg[pb_]])
                for d in range(2):
                    S.op("pool", lambda e, o=v8(dtx[:, d, :]), i=v8(xs_tok), s_=bc_mid(dtv_all[:, c, d * 8:d * 8 + 8], 64): e.tensor_tensor(out=o, in0=i, in1=s_, op=ALU.mult),
                         reads=[t_xs[pb_], t_dt[c]], writes=[t_dtx[pb_]])
                    S.op("pool", lambda e, o=v8(Wd[:, d, :]), i=v8(xs_tok), s_=bc_mid(wgt[:, d * 8:d * 8 + 8], 64): e.tensor_tensor(out=o, in0=i, in1=s_, op=ALU.mult),
                         reads=[t_xs[pb_], t_wg[pb_]], writes=[t_W[pb_]])
                S.op("dve", lambda e, o=v8(y_part[:, c, :]), i=v8(xs_tok), s_=bc_mid(dsk, 64): e.tensor_tensor(out=o, in0=i, in1=s_, op=ALU.mult), reads=[t_xs[pb_], t_dtp], writes=[t_yp[c]])

            def m2_back(c):
                pb_ = c % 2
                MT, B_tok, dtx, Wd = MTb[pb_], Btb[pb_], dtxb[pb_], Wdb[pb_]
                cols = slice(c * 128, (c + 1) * 128)
                for d in range(2):
                    bk = 6 + d
                    for h in range(8):
                        mm(banks[bk][:, h * 64:(h + 1) * 64], MT[:, d * 8 + h, :], dtx[:, d, h * 64:(h + 1) * 64], True, True, [t_MT[pb_], t_dtx[pb_]],
                           ptok[bk] if h == 7 else None, ptok[bk] if h == 0 else None)
                for d in range(2):
                    bk = 4 + d
                    for g in range(2):
                        mm(banks[bk][:, g * 256:(g + 1) * 256], B_tok[:, g * 128:(g + 1) * 128], Wd[:, d, g * 256:(g + 1) * 256], True, True, [t_Bt[pb_], t_W[pb_]],
                           ptok[bk] if g == 1 else None, ptok[bk] if g == 0 else None)
                byo = 0
                for g in range(2):
                    mm(banks[byo][:, g * 256:(g + 1) * 256], xbcT[:, 6 + g, cols], Sbf[:, g * 256:(g + 1) * 256], True, True, [t_xbc[6 + g], t_Sbf],
                       ptok[byo] if g == 1 else None, ptok[byo] if g == 0 else None)
                S.op("dve", lambda e, o=v8(Sn), i=v8(Sst), s_=bc_mid(E_all[:, c, 32:40], 64): e.tensor_tensor(out=o, in0=i, in1=s_, op=ALU.mult), reads=[t_S, t_E[c]], writes=[t_Sn])
                S.op("dve", lambda e, i=banks[4][:, :]: e.tensor_tensor(out=Sn, in0=Sn, in1=i, op=ALU.add), reads=[ptok[4]], writes=[t_Sn])
                if c < NT - 1:
                    if c + 1 == 8:
                        S.op("dve", lambda e: e.memset(Sst, 0.0), writes=[t_S])
                    elif (c + 1) % 2 == 0:
                        S.op("dve", lambda e: e.tensor_scalar_mul(out=Sst, in0=Sn, scalar1=samp), reads=[t_Sn, t_const], writes=[t_S])
                    else:
                        S.op("dve", lambda e: e.tensor_copy(out=Sst, in_=Sn), reads=[t_Sn], writes=[t_S])
                    S.op("act", lambda e: e.copy(out=Sbf, in_=Sst), reads=[t_S], writes=[t_Sbf])
                S.op("dve", lambda e, o=v8(ytmp), i=v8(banks[byo][:, :]), s_=bc_mid(E_all[:, c, 0:8], 64): e.tensor_tensor(out=o, in0=i, in1=s_, op=ALU.mult),
                     reads=[ptok[byo], t_E[c]], writes=[t_y1])
                S.op("dve", lambda e, i=banks[6][:, :]: e.tensor_tensor(out=ytmp, in0=ytmp, in1=i, op=ALU.add), reads=[ptok[6]], writes=[t_y1])
                S.op("dve", lambda e, i=banks[7][:, :]: e.tensor_tensor(out=ytmp, in0=ytmp, in1=i, op=ALU.add), reads=[ptok[7]], writes=[t_y1])
                S.op("dve", lambda e, o=y_part[:, c, :]: e.tensor_tensor(out=o, in0=ytmp, in1=o, op=ALU.add), reads=[t_y1], writes=[t_yp[c]])
                S.op("act", lambda e, o=cs_b[:, c, :], i=banks[5][:, :]: e.copy(out=o, in_=i), reads=[ptok[5]], writes=[t_csb[c]])
                if c % 2 == 1:
                    store_state(Sn, t_Sn, c // 2, 0, byo)

            m2_front(0)
            for c in range(NT):
                if c + 1 < NT:
                    m2_front(c + 1)
                m2_back(c)
            S.flush()
            otk = A16(117, 512)
            yz = A32(150, 512)
            ytmp2 = A32(156, 512)
            t_y2, t_y3 = Tok(), Tok()
            ssq = small[:, 248:249]
            t_otk, t_yz, t_ssq = Tok(), Tok(), Tok()
            S.op("dve", lambda e: e.memset(Sst, 0.0), writes=[t_S])
            S.op("dve", lambda e: e.memset(Sbf, 0.0), writes=[t_Sbf])
            for c in range(NT - 1, -1, -1):
                cols = slice(c * 128, (c + 1) * 128)
                bk = nextbank()
                for g in range(2):
                    mm(banks[bk][:, g * 256:(g + 1) * 256], xbcT[:, 6 + g, cols], Sbf[:, g * 256:(g + 1) * 256], True, True, [t_xbc[6 + g], t_Sbf],
                       ptok[bk] if g == 1 else None, ptok[bk] if g == 0 else None)
                S.op("dve", lambda e, o=v8(Sn), i=v8(Sst), s_=bc_mid(E_all[:, c, 40:48], 64): e.tensor_tensor(out=o, in0=i, in1=s_, op=ALU.mult), reads=[t_S, t_E[c]], writes=[t_Sn])
                S.op("dve", lambda e, i=cs_b[:, c, :]: e.tensor_tensor(out=Sn, in0=Sn, in1=i, op=ALU.add), reads=[t_csb[c]], writes=[t_Sn])
                if c % 2 == 0:
                    store_state(Sn, t_Sn, c // 2, 1, nextbank())
                if c > 0:
                    if c == 8:
                        load_state(1, Sst, t_S)
                    elif c % 2 == 0:
                        S.op("dve", lambda e: e.tensor_scalar_mul(out=Sst, in0=Sn, scalar1=samp), reads=[t_Sn, t_const], writes=[t_S])
                    else:
                        S.op("dve", lambda e: e.tensor_copy(out=Sst, in_=Sn), reads=[t_Sn], writes=[t_S])
                    S.op("act", lambda e: e.copy(out=Sbf, in_=Sst), reads=[t_S], writes=[t_Sbf])
                S.op("dve", lambda e, o=v8(ytmp), i=v8(banks[bk][:, :]), s_=bc_mid(E_all[:, c, 8:16], 64): e.tensor_tensor(out=o, in0=i, in1=s_, op=ALU.mult),
                     reads=[ptok[bk], t_E[c]], writes=[t_y3])
                S.op("dve", lambda e, i=y_part[:, c, :]: e.tensor_tensor(out=ytmp, in0=ytmp, in1=i, op=ALU.add), reads=[t_yp[c]], writes=[t_y3])
                S.op("pool", lambda e, i=z_tok[:, c, :]: e.tensor_tensor(out=yz, in0=ytmp, in1=i, op=ALU.mult), reads=[t_y3, t_z[c]], writes=[t_yz])
                S.op("act", lambda e: e.activation(out=ytmp2, in_=yz, func=AF.Square), reads=[t_yz], writes=[t_y2])
                S.op("dve", lambda e: e.reduce_sum(out=ssq, in_=ytmp2, axis=AX.X), reads=[t_y2], writes=[t_ssq])
                S.op("act", lambda e: e.activation(out=ssq, in_=ssq, func=AF.Sqrt, scale=1.0 / 512, bias=epsc), reads=[t_const], writes=[t_ssq])
                S.op("dve", lambda e: e.reciprocal(out=ssq, in_=ssq), writes=[t_ssq])
                S.op("dve", lambda e: e.scalar_tensor_tensor(out=otk, in0=yz, scalar=ssq, in1=gssm, op0=ALU.mult, op1=ALU.mult), reads=[t_yz, t_ssq, t_gs], writes=[t_otk])
                bt = nextbank()
                for j in range(4):
                    tr(bankbf(bt)[:, j * 128:(j + 1) * 128], otk[:, j * 128:(j + 1) * 128], identb, [t_otk], ptok[bt], ptok[bt], j == 3, j == 0)
                S.op("act", lambda e, o=oT[:, 8:12, cols], i=bankbf(bt)[:, 0:512].rearrange("p (j c) -> p j c", j=4): e.copy(out=o, in_=i), reads=[ptok[bt]], writes=t_oT[8:12])
            S.flush()

        def MIXERS(l):
            if "attn" in skip:
                for fc in range(0, 4):
                    S.op("dve", lambda e, o=oT[:, fc, :]: e.memset(o, 0.0), writes=[t_oT[fc]])
            else:
                stage_attn(l)
            if "fnet" in skip:
                for fc in range(4, 8):
                    S.op("dve", lambda e, o=oT[:, fc, :]: e.memset(o, 0.0), writes=[t_oT[fc]])
            else:
                stage_fourier(l)
            if "ssm" in skip:
                for fc in range(8, 12):
                    S.op("dve", lambda e, o=oT[:, fc, :]: e.memset(o, 0.0), writes=[t_oT[fc]])
            else:
                stage_ssm(l)
            if "gmlp" in skip:
                for fc in range(12, 16):
                    S.op("dve", lambda e, o=oT[:, fc, :]: e.memset(o, 0.0), writes=[t_oT[fc]])
            else:
                stage_gmlp(l)
            for nm in ("att", "fnet", "ssm", "gmlp"):
                i0 = {"att": 0, "fnet": 4, "ssm": 8, "gmlp": 12}[nm]
                tap("o_" + nm, oT[:, i0:i0 + 4, :], t_oT[i0:i0 + 4])
            S.flush()

        order = ["stage0", "mod", "norm1", "mixers", "merge", "wout", "norm2", "ffn"]
        lim = order.index(upto) if upto else 99
        stage0()
        for l in range(nlayers):
            if l == 0:
                mod_drain(2)
                S.flush()
            if l + 1 < nlayers:
                mod_q.extend((l + 1, j) for j in range(96))
            if lim >= 2:
                stage_norm(l, 0)
                spill_x()
                S.flush()
            if lim >= 3:
                MIXERS(l)
            if lim >= 4:
                stage_merge(l)
            if lim >= 5:
                stage_wffn(l, lim)
            mod_drain(len(mod_q) if l + 1 < nlayers else 0)
            S.flush()

        ytok = [A32(0 + 8 * i, 2048) for i in range(2)]
        t_ytok = [Tok(), Tok()]
        for t in range(NT):
            b = t % 2
            for q4 in range(4):
                bk = nextbank()
                for j in range(4):
                    fc = q4 * 4 + j
                    tr(banks[bk][:, j * 128:(j + 1) * 128], xT[:, fc, t * 128:(t + 1) * 128], ident_f,
                       [t_xT[fc]], ptok[bk], ptok[bk], j == 3, j == 0)
                copy_any(alt(), ytok[b][:, q4 * 512:(q4 + 1) * 512], banks[bk][:, :], [ptok[bk]], [t_ytok[b]])
            S.dma("sp", y_o[t * 128:(t + 1) * 128, :], ytok[b], reads=[t_ytok[b]])
        S.flush()
    return nc


_NC_CACHE = {}


def _tables():
    half = 16
    freqs = 10000.0 ** (-np.arange(half, dtype=np.float64) / half)
    l = np.arange(1024)
    rows = (l // 64).astype(np.float64)
    cols = (l % 64).astype(np.float64)
    ar = rows[:, None] * freqs[None, :]
    ac = cols[:, None] * freqs[None, :]
    cos_s = np.concatenate([np.cos(ar), np.cos(ar), np.cos(ac), np.cos(ac)], axis=1)
    sin_s = np.concatenate([-np.sin(ar), np.sin(ar), -np.sin(ac), np.sin(ac)], axis=1)
    cos_p = np.ones((1024, 64))
    sin_p = np.zeros((1024, 64))
    cosB = np.ones((256, 64))
    sinB = np.zeros((256, 64))
    rope_s = (np.concatenate([cos_s, cosB]).astype(np.float32), np.concatenate([sin_s, sinB]).astype(np.float32))
    rope_p = (np.concatenate([cos_p, cosB]).astype(np.float32), np.concatenate([sin_p, sinB]).astype(np.float32))

    def dft(L):
        a = 2.0 * np.pi * np.outer(np.arange(L), np.arange(L)) / L
        s = 1.0 / np.sqrt(L * 128.0)
        return np.cos(a) * s, -np.sin(a) * s

    c1024, s1024 = dft(1024)
    c256, s256 = dft(256)
    dftA_s = np.stack([c1024, s1024]).astype(np.float32)
    cb = np.zeros((1024, 1024))
    sb = np.zeros((1024, 1024))
    for i in range(4):
        cb[i * 256:(i + 1) * 256, i * 256:(i + 1) * 256] = c256
        sb[i * 256:(i + 1) * 256, i * 256:(i + 1) * 256] = s256
    dftA_p = np.stack([cb, sb]).astype(np.float32)
    dftB = np.stack([c256, s256]).astype(np.float32)
    a = 2.0 * np.pi * np.outer(np.arange(128), np.arange(128)) / 128.0
    dftC = np.concatenate([np.cos(a), np.sin(a)], axis=1).astype(np.float32)
    BIG = 30000.0
    km_s = np.zeros((4, 1536), np.float32)
    qm_s = np.zeros((4, T), np.float32)
    km_p = np.zeros((4, 1536), np.float32)
    qm_p = np.zeros((4, T), np.float32)
    km_p[:, 0:256] = 1.0
    for s in range(4):
        km_p[s, 256 + s * 256:256 + (s + 1) * 256] = 1.0
        qm_p[:, s * 256:(s + 1) * 256] = -BIG
        qm_p[s, s * 256:(s + 1) * 256] = 0.0
    k = np.arange(128)[:, None]
    i = np.arange(128)[None, :]
    tri = np.stack([(k <= i), (k > i), (k >= i), (k < i)]).astype(np.float32)
    ident = np.eye(128, dtype=np.float32)
    return dict(rope_s=rope_s, rope_p=rope_p, dftA_s=dftA_s, dftA_p=dftA_p, dftB=dftB, dftC=dftC,
                km_s=km_s, qm_s=qm_s, km_p=km_p, qm_p=qm_p, tri=tri, ident=ident)


def _prompt_ids(core):
    if core < 2:
        return [], core
    base = 2 + (core - 2) * 5
    return [base, base + 1, base + 2, base + 3], base + 4


def make_in_maps(inp):
    tb = _tables()
    f = lambda a: np.ascontiguousarray(np.asarray(a, dtype=np.float32))
    x_prompt = f(inp["x_prompt"])
    x_sample = f(inp["x_sample"])
    c = f(inp["c"])
    c_ctx = f(inp["c_ctx"])
    cache_k = f(inp["cache_k"])
    cache_v = f(inp["cache_v"])
    state = f(inp["state_ssm"])
    shared = {
        "dftB": tb["dftB"], "dftC": tb["dftC"], "ident": tb["ident"], "tri": tb["tri"],
        "w_mod": f(inp["w_mod"]), "b_mod": f(inp["b_mod"]), "w_in": f(inp["w_in"]),
        "q_norm_g": f(inp["q_norm_g"]), "k_norm_g": f(inp["k_norm_g"]),
        "conv_w": f(inp["conv_w"]), "conv_b": f(inp["conv_b"]),
        "a_log": f(inp["a_log"]).reshape(2, 16), "dt_bias": f(inp["dt_bias"]).reshape(2, 16),
        "d_skip": f(inp["d_skip"]), "ssm_norm_g": f(inp["ssm_norm_g"]), "gmlp_norm_g": f(inp["gmlp_norm_g"]),
        "w_spatial": f(inp["w_spatial"]), "b_spatial": f(inp["b_spatial"]).reshape(2, 512),
        "w_branch": f(inp["w_branch"]), "w_out": f(inp["w_out"]), "w_ff1": f(inp["w_ff1"]), "w_ff2": f(inp["w_ff2"]),
    }
    maps = []
    for core in range(8):
        a_ids, b_id = _prompt_ids(core)
        m = dict(shared)
        if core < 2:
            xa = x_sample[core]
            conds = np.stack([c[core], c_ctx])
            m["ctxk"] = np.ascontiguousarray(cache_k[core].reshape(2, 256, 128))
            m["ctxv"] = np.ascontiguousarray(cache_v[core].reshape(2, 256, 128))
            m["h0"] = np.ascontiguousarray(state[core].reshape(2, 2, 512, 128))
            m["flags"] = np.tile(np.array([[1.0, 0.0, 0.0, 0.0]], np.float32), (128, 1))
            m["ropec"], m["ropes"] = tb["rope_s"]
            m["kmask"], m["qmask"] = tb["km_s"], tb["qm_s"]
            m["dftA"] = tb["dftA_s"]
        else:
            xa = x_prompt[a_ids].reshape(1024, D)
            conds = np.stack([c_ctx, c_ctx])
            m["ctxk"] = np.zeros((2, 256, 128), np.float32)
            m["ctxv"] = np.zeros((2, 256, 128), np.float32)
            m["h0"] = np.zeros((2, 2, 512, 128), np.float32)
            m["flags"] = np.tile(np.array([[0.0, 1.0, 0.0, 0.0]], np.float32), (128, 1))
            m["ropec"], m["ropes"] = tb["rope_p"]
            m["kmask"], m["qmask"] = tb["km_p"], tb["qm_p"]
            m["dftA"] = tb["dftA_p"]
        m["xin"] = np.ascontiguousarray(np.concatenate([xa, x_prompt[b_id]], axis=0))
        m["cond"] = np.ascontiguousarray(conds)
        maps.append(m)
    return maps


def assemble(results):
    yp = np.zeros((32, 256, D), np.float32)
    ys = np.zeros((2, 1024, D), np.float32)
    nk = np.zeros((32, 2, 256, 2, 64), np.float32)
    nv = np.zeros((32, 2, 256, 2, 64), np.float32)
    ns = np.zeros((32, 2, 2, 8, 64, 128), np.float32)
    for core in range(8):
        r = results[core]
        a_ids, b_id = _prompt_ids(core)
        y = np.asarray(r["y"])
        k_ = np.asarray(r["nk"])
        v_ = np.asarray(r["nv"])
        s_ = np.asarray(r["ns"])
        if core < 2:
            ys[core] = y[0:1024]
        seqs = [(pid, i * 256, i) for i, pid in enumerate(a_ids)] + [(b_id, 1024, 4)]
        for pid, r0, si in seqs:
            yp[pid] = y[r0:r0 + 256]
            nk[pid] = k_[:, r0:r0 + 256, :].reshape(2, 256, 2, 64)
            nv[pid] = v_[:, r0:r0 + 256, :].reshape(2, 256, 2, 64)
            ns[pid] = s_[:, si].reshape(2, 2, 8, 64, 128)
    return yp, ys, nk, nv, ns


def kernel(**inputs):
    if "nc" not in _NC_CACHE:
        _NC_CACHE["nc"] = build()
    nc = _NC_CACHE["nc"]
    in_maps = make_in_maps(inputs)
    res = run_bass_kernel_spmd(nc, in_maps, core_ids=list(range(8)))
    return assemble(res.results)
```
